# Optimizing a Trainium2 kernel written in Bass

```python
import jax, jax.numpy as jnp
from jax import lax
import numpy as np

D_MODEL = 1024
BATCH = 1
SEQ = 16384
DEPTH = 4
DEC_BATCH = 32
DEC_SEQ = 2048
PAST_LEN = 128

HEAD_DIM = D_MODEL // 16
N_HEADS = 8
N_KV_HEADS = 2
GQA_GROUP = N_HEADS // N_KV_HEADS
WINDOW = 128
BLOCK = 128
N_FOURIER_GROUPS = 4
FOURIER_GROUP_WIDTH = D_MODEL // 16
N_MEM = 256
N_MEM_HEADS = 4
ATTN_WIDTH = N_HEADS * HEAD_DIM
KV_WIDTH = N_KV_HEADS * HEAD_DIM
FOURIER_WIDTH = N_FOURIER_GROUPS * FOURIER_GROUP_WIDTH
MEM_WIDTH = N_MEM_HEADS * HEAD_DIM
MIX_WIDTH = ATTN_WIDTH + FOURIER_WIDTH + MEM_WIDTH
IN_WIDTH = ATTN_WIDTH + 2 * KV_WIDTH + FOURIER_WIDTH + MEM_WIDTH
D_FF = 4 * D_MODEL
EPS = 1e-6
NEG_INF = -1e30

kernel_name = "hymba_style_window_gqa_fnet_memory_encoder"


def rmsnorm(x, g):
    xf = x.astype(jnp.float32)
    y = xf * lax.rsqrt(jnp.mean(xf * xf, axis=-1, keepdims=True) + EPS)
    return (y * g.astype(jnp.float32)).astype(x.dtype)


def alibi_slopes():
    return jnp.exp2(-8.0 * jnp.arange(1, N_HEADS + 1, dtype=jnp.float32) / N_HEADS)


def window_gqa_attention(q, k, v, sinks):
    B, S, _ = q.shape
    nb = S // BLOCK
    qb = q.reshape(B, nb, BLOCK, N_KV_HEADS, GQA_GROUP, HEAD_DIM)
    k = k.reshape(B, S, N_KV_HEADS, HEAD_DIM)
    v = v.reshape(B, S, N_KV_HEADS, HEAD_DIM)
    pad = ((0, 0), (BLOCK, BLOCK), (0, 0), (0, 0))
    kp = jnp.pad(k, pad)
    vp = jnp.pad(v, pad)
    kb = jnp.concatenate([kp[:, i * BLOCK:i * BLOCK + S].reshape(B, nb, BLOCK, N_KV_HEADS, HEAD_DIM) for i in range(3)], axis=2)
    vb = jnp.concatenate([vp[:, i * BLOCK:i * BLOCK + S].reshape(B, nb, BLOCK, N_KV_HEADS, HEAD_DIM) for i in range(3)], axis=2)
    scale = HEAD_DIM ** -0.5
    scores = jnp.einsum('bnqkgd,bnckd->bnkgqc', qb, kb).astype(jnp.float32) * scale
    a = jnp.arange(BLOCK)
    c = jnp.arange(3 * BLOCK)
    rel = c[None, :] - BLOCK - a[:, None]
    dist = jnp.abs(rel).astype(jnp.float32)
    key_pos = (jnp.arange(nb)[:, None] - 1) * BLOCK + c[None, :]
    in_range = (key_pos >= 0) & (key_pos < S)
    mask = (jnp.abs(rel) <= WINDOW)[None, :, :] & in_range[:, None, :]
    bias = -alibi_slopes().reshape(N_KV_HEADS, GQA_GROUP, 1, 1) * dist
    scores = jnp.where(mask[None, :, None, None, :, :], scores + bias, NEG_INF)
    sink = jnp.broadcast_to(sinks.astype(jnp.float32).reshape(1, 1, N_KV_HEADS, GQA_GROUP, 1, 1), scores.shape[:-1] + (1,))
    probs = jax.nn.softmax(jnp.concatenate([scores, sink], axis=-1), axis=-1)[..., :-1]
    out = jnp.einsum('bnkgqc,bnckd->bnqkgd', probs.astype(v.dtype), vb)
    return out.reshape(B, S, ATTN_WIDTH)


def fourier_mix(u):
    B, S, _ = u.shape
    uf = u.astype(jnp.float32).reshape(B, S, N_FOURIER_GROUPS, FOURIER_GROUP_WIDTH)
    f = jnp.fft.fftn(uf, axes=(1, 3), norm='ortho').real
    return f.reshape(B, S, FOURIER_WIDTH).astype(u.dtype)


def memory_cross_attention(qm, mem_n, w_mem_kv):
    B, S, _ = qm.shape
    kv = mem_n @ w_mem_kv
    km, vm = jnp.split(kv, 2, axis=-1)
    km = km.reshape(B, N_MEM, N_MEM_HEADS, HEAD_DIM)
    vm = vm.reshape(B, N_MEM, N_MEM_HEADS, HEAD_DIM)
    q = qm.reshape(B, S, N_MEM_HEADS, HEAD_DIM)
    scores = jnp.einsum('bshd,bmhd->bhsm', q, km).astype(jnp.float32) * (HEAD_DIM ** -0.5)
    probs = jax.nn.softmax(scores, axis=-1)
    out = jnp.einsum('bhsm,bmhd->bshd', probs.astype(vm.dtype), vm)
    return out.reshape(B, S, MEM_WIDTH)


def group_output_norm(y_attn, y_four, y_mem, g):
    g_a, g_f, g_m = jnp.split(g, [ATTN_WIDTH, ATTN_WIDTH + FOURIER_WIDTH])
    return jnp.concatenate([rmsnorm(y_attn, g_a), rmsnorm(y_four, g_f), rmsnorm(y_mem, g_m)], axis=-1)


def encoder_layer(x, mem, g_mix, w_in, g_mem, w_mem_kv, sinks, g_grp, w_out, g_ffn, w_ff1, w_ff2):
    h = rmsnorm(x, g_mix)
    proj = h @ w_in
    o1 = ATTN_WIDTH
    o2 = o1 + KV_WIDTH
    o3 = o2 + KV_WIDTH
    o4 = o3 + FOURIER_WIDTH
    q, k, v, uf, qm = jnp.split(proj, [o1, o2, o3, o4], axis=-1)
    y_attn = window_gqa_attention(q, k, v, sinks)
    y_four = fourier_mix(uf)
    y_mem = memory_cross_attention(qm, rmsnorm(mem, g_mem), w_mem_kv)
    x = x + group_output_norm(y_attn, y_four, y_mem, g_grp) @ w_out
    hf = rmsnorm(x, g_ffn) @ w_ff1
    x = x + jnp.square(jax.nn.relu(hf)) @ w_ff2
    return x


def trunk(x, mem, g_mix, w_in, g_mem, w_mem_kv, sinks, g_grp, w_out, g_ffn, w_ff1, w_ff2, g_final):
    for l in range(DEPTH):
        x = encoder_layer(x, mem, g_mix[l], w_in[l], g_mem[l], w_mem_kv[l], sinks[l], g_grp[l], w_out[l], g_ffn[l], w_ff1[l], w_ff2[l])
    return rmsnorm(x, g_final)


def setup_inputs(seed: int = 0) -> dict:
    key = jax.random.key(seed)
    ks = jax.random.split(key, 16)
    f32 = jnp.float32

    def gain(k, shape):
        return 1.0 + 0.02 * jax.random.normal(k, shape, f32)

    return {
        'x_prompt': jax.random.normal(ks[0], (BATCH, SEQ, D_MODEL), f32),
        'x_sample': jax.random.normal(ks[1], (DEC_BATCH, DEC_SEQ, D_MODEL), f32),
        'mem_prompt': jax.random.normal(ks[2], (BATCH, N_MEM, D_MODEL), f32),
        'mem_sample': jax.random.normal(ks[3], (DEC_BATCH, N_MEM, D_MODEL), f32),
        'g_mix': gain(ks[4], (DEPTH, D_MODEL)),
        'w_in': jax.random.normal(ks[5], (DEPTH, D_MODEL, IN_WIDTH), f32) * D_MODEL ** -0.5,
        'g_mem': gain(ks[6], (DEPTH, D_MODEL)),
        'w_mem_kv': jax.random.normal(ks[7], (DEPTH, D_MODEL, 2 * MEM_WIDTH), f32) * D_MODEL ** -0.5,
        'sinks': 0.5 * jax.random.normal(ks[8], (DEPTH, N_HEADS), f32),
        'g_grp': gain(ks[9], (DEPTH, MIX_WIDTH)),
        'w_out': jax.random.normal(ks[10], (DEPTH, MIX_WIDTH, D_MODEL), f32) * MIX_WIDTH ** -0.5,
        'g_ffn': gain(ks[11], (DEPTH, D_MODEL)),
        'w_ff1': jax.random.normal(ks[12], (DEPTH, D_MODEL, D_FF), f32) * D_MODEL ** -0.5,
        'w_ff2': jax.random.normal(ks[13], (DEPTH, D_FF, D_MODEL), f32) * D_FF ** -0.5,
        'g_final': gain(ks[14], (D_MODEL,)),
    }


def reference(x_prompt, x_sample, mem_prompt, mem_sample, g_mix, w_in, g_mem, w_mem_kv, sinks, g_grp, w_out, g_ffn, w_ff1, w_ff2, g_final):
    y_prompt = trunk(x_prompt, mem_prompt, g_mix, w_in, g_mem, w_mem_kv, sinks, g_grp, w_out, g_ffn, w_ff1, w_ff2, g_final)
    y_sample = trunk(x_sample, mem_sample, g_mix, w_in, g_mem, w_mem_kv, sinks, g_grp, w_out, g_ffn, w_ff1, w_ff2, g_final)
    return (y_prompt, y_sample)
```

```python
import os
import types
import numpy as np
import ml_dtypes
import concourse.bass as bass
import concourse.mybir as mybir
from concourse.bass_utils import run_bass_kernel_spmd

F32 = mybir.dt.float32
BF16 = mybir.dt.bfloat16
ALU = mybir.AluOpType
AF = mybir.ActivationFunctionType
AX = mybir.AxisListType
NPBF = ml_dtypes.bfloat16

D = 1024
S = 2048
NCORE = 8
EPS = 1e-6
NEG = -30000.0
HALO = 384
AGR = S + HALO
ENGS = ["pe", "act", "dve", "pool", "sp"]
NDSEM = 24
KSTOP = os.environ.get('KSTOP', '')
KCUT = int(os.environ.get('KCUT', '0'))
KLOG = os.environ.get('KLOG', '')


class Res:
    __slots__ = ("w", "rs")

    def __init__(self):
        self.w = None
        self.rs = []


class Op:
    __slots__ = ("eng", "idx", "fn", "deps", "dma", "sig", "cnt", "known", "semidx", "csem", "ov")


class Rec:
    def __init__(self):
        self.q = {e: [] for e in ENGS}
        self.known = {e: {f: -1 for f in ENGS} for e in ENGS}
        self.wd = {e: set() for e in ENGS}
        self.dmas = []
        self.ovd = []

    def mark(self, name):
        if KLOG:
            print("MARK", name, getattr(self, "nops", 0), flush=True)

    def op(self, eng, fn, r=(), w=(), dma=False, csem=None, ov=False, extra=()):
        self.nops = getattr(self, "nops", 0) + 1
        if KCUT and self.nops > KCUT:
            return None
        extra = [e_ for e_ in extra if e_ is not None]
        if fn is not None and fn.__closure__:
            cells = []
            for c in fn.__closure__:
                try:
                    cells.append(types.CellType(c.cell_contents))
                except ValueError:
                    cells.append(c)
            f2 = types.FunctionType(fn.__code__, fn.__globals__, fn.__name__, fn.__defaults__, tuple(cells))
            f2.__kwdefaults__ = fn.__kwdefaults__
            fn = f2
        o = Op()
        o.eng = eng; o.idx = len(self.q[eng]); o.fn = fn; o.dma = dma or (csem is not None)
        o.sig = False; o.deps = []; o.cnt = 0; o.csem = csem; o.semidx = -1; o.ov = ov
        cand = list(extra)
        for x in r:
            if x.w is not None:
                cand.append(x.w)
        for x in w:
            if x.w is not None:
                cand.append(x.w)
            cand.extend(x.rs)
        if csem is not None:
            o.cnt = 1; o.sig = True
        elif dma:
            n = len(self.dmas)
            o.semidx = n % NDSEM; o.cnt = 16 * (n // NDSEM + 1)
            if n >= NDSEM:
                cand.append(self.dmas[n - NDSEM])
            self.dmas.append(o); o.sig = True
        if o.dma and ov:
            self.ovd.append(o)
        kn = self.known[eng]
        best = {}
        for d in cand:
            if d is o:
                continue
            if d.dma:
                if id(d) not in self.wd[eng]:
                    self.wd[eng].add(id(d)); o.deps.append(d)
                    for f, v in d.known.items():
                        if v > kn[f]:
                            kn[f] = v
            else:
                if d.eng == eng and eng in ("pe", "sp"):
                    continue
                if kn[d.eng] >= d.idx:
                    continue
                if d.eng not in best or best[d.eng].idx < d.idx:
                    best[d.eng] = d
        for d in best.values():
            if kn[d.eng] >= d.idx:
                continue
            d.sig = True; o.deps.append(d)
            kn[d.eng] = d.idx
            for f, v in d.known.items():
                if v > kn[f]:
                    kn[f] = v
        o.known = dict(kn)
        self.q[eng].append(o)
        for x in w:
            x.w = o; x.rs = []
        for x in r:
            if x.w is not o:
                x.rs.append(o)
        return o

    def barrier(self):
        ex = []
        for f in ENGS:
            for o in reversed(self.q[f]):
                if (not o.dma) and o.fn is not None:
                    ex.append(o); break
        pend = list(self.ovd); self.ovd = []
        for e in ENGS:
            self.op(e, None, extra=ex + pend)


def _bf(a):
    return np.ascontiguousarray(a.astype(NPBF))


def host_tables(n_prompt_tok):
    H = 8
    slopes = np.exp2(-8.0 * np.arange(1, H + 1, dtype=np.float64) / H)
    key = np.arange(128)[:, None]; q = np.arange(128)[None, :]
    bias = np.zeros((128, 3, 2, 512), np.float32)
    for o in range(3):
        rel = key + (o - 1) * 128 - q
        ok = np.abs(rel) <= 128
        for hb in range(2):
            for g in range(2):
                for e in range(2):
                    h = 4 * g + 2 * e + hb
                    b = np.where(ok, -slopes[h] * np.abs(rel), NEG)
                    bias[:, o, hb, (g * 2 + e) * 128:(g * 2 + e + 1) * 128] = b
    bias = bias.reshape(128, 3 * 2 * 512)
    c = np.arange(64)
    ang = 2 * np.pi * np.outer(c, c) / 64.0
    C64 = np.cos(ang) / 8.0; S64 = np.sin(ang) / 8.0
    cs = np.zeros((128, 2, 128), np.float64)
    for gg in range(2):
        cs[gg * 64:(gg + 1) * 64, 0, gg * 64:(gg + 1) * 64] = C64
        cs[gg * 64:(gg + 1) * 64, 1, gg * 64:(gg + 1) * 64] = -S64
    cs = _bf(cs.reshape(128, 256))

    def dtab(ntot, k0):
        n = np.arange(ntot, dtype=np.int64)[:, None]
        out = np.empty((2, 8, ntot, 256), NPBF)
        for kc in range(8):
            k = (k0 + kc * 256 + np.arange(256, dtype=np.int64))[None, :]
            a = 2 * np.pi * ((n * k) % ntot).astype(np.float64) / ntot
            out[0, kc] = (np.cos(a) / np.sqrt(ntot)).astype(NPBF)
            out[1, kc] = (np.sin(a) / np.sqrt(ntot)).astype(NPBF)
        return out
    tab_s = dtab(S, 0)
    tab_p = [dtab(n_prompt_tok, c_ * S) for c_ in range(NCORE)]
    return bias, cs, tab_s, tab_p


def build(L, NS, NPB):
    nc = bass.Bass("TRN2", target_bir_lowering=False)
    R = Rec()
    NU = 1 + NS
    di = lambda n, s, dt=F32: nc.dram_tensor(n, s, dt, kind="ExternalInput").ap()
    xp = di("xp", [S, D]); xs = di("xs", [NS * S, D]); memp = di("memp", [256, D]); mems = di("mems", [NS * 256, D])
    LW = max(L, 1)
    w_in = di("w_in", [LW, D, 1280]); w_kv = di("w_kv", [LW, D, 512]); w_out = di("w_out", [LW, D, D])
    w_f1 = di("w_f1", [LW, D, 4096]); w_f2 = di("w_f2", [LW, 4096, D])
    NG = 4 * L * 8 + 8
    gall = di("gall", [128, NG]); sinks_d = di("sinks", [128, LW * 8]); oh_d = di("oh", [128, 18])
    bias_d = di("bias", [128, 3072]); ident_d = di("ident", [128, 128])
    cs_d = di("cs64", [128, 256], BF16)
    tabs_d = di("tab_s", [2, 8, S, 256], BF16); tabp_d = di("tab_p", [2, 8, NPB * 128, 256], BF16)
    yp = nc.dram_tensor("yp", [S, D], F32, kind="ExternalOutput").ap()
    ys = nc.dram_tensor("ys", [NS * S, D], F32, kind="ExternalOutput").ap()
    dn = lambda n, s: nc.dram_tensor(n, s, BF16, kind="Internal").ap()
    wb_in = dn("wb_in", [LW, D, 1408]); wb_kv = dn("wb_kv", [LW, D, 512]); wb_out = dn("wb_out", [LW, D, D])
    wb_f1 = dn("wb_f1", [LW, D, 4096]); wb_f2 = dn("wb_f2", [LW, 4096, D])
    ag_in = [dn(f"ag_in{l}", [AGR, 256]) for l in range(L)]
    ag_out = [dn(f"ag_out{l}", [NCORE * AGR, 256]) for l in range(L)]

    import contextlib
    es = contextlib.ExitStack()
    with es:
        sb = lambda n, s, dt: es.enter_context(nc.sbuf_tensor(n, s, dt))
        x_sb = sb("x_sb", [128, 8, S], F32)
        wa = sb("wa", [128, 8, 1408], BF16)
        wbt = sb("wbt", [128, 8, 1024], BF16)
        bias_sb = sb("bias_sb", [128, 3, 2, 512], F32)
        ident_f = sb("ident_f", [128, 128], F32)
        ident_b = sb("ident_b", [128, 128], BF16)
        ones_b = sb("ones_b", [128, 128], BF16)
        cs_sb = sb("cs_sb", [128, 2, 128], BF16)
        g_sb = sb("g_sb", [128, NG], F32)
        sink_sb = sb("sink_sb", [128, LW * 8], F32)
        oh_sb = sb("oh_sb", [128, 18], F32)
        small = sb("small", [128, 8], F32)
        OVN = 91 * 512
        ov = sb("ov", [128, OVN], BF16)
        ps = es.enter_context(nc.psum_tensor("ps", [128, 4096], F32))
        esem = {e: es.enter_context(nc.semaphore("s_" + e)) for e in ENGS}
        dsem = [es.enter_context(nc.semaphore(f"d{i}")) for i in range(NDSEM)]
        csems = [es.enter_context(nc.semaphore(f"cc{l}")) for l in range(L)]
        block = es.enter_context(nc.Block())

        K = 512

        def carve(off_k, nbytes_k, dt, pat=None, **kw):
            a = ov[:, int(off_k * K):int((off_k + nbytes_k) * K)]
            if dt == F32:
                a = a.bitcast(F32)
            if pat:
                a = a.rearrange(pat, **kw)
            return a
        hq = carve(0, 8, BF16, "p (c t) -> p c t", c=8)
        q_sb = carve(8, 16, BF16, "p (i c t) -> p i c t", i=4, c=4)
        qm_sb = carve(24, 8, BF16, "p (i c t) -> p i c t", i=4, c=2)
        kd = carve(32, 9, BF16, "p (g t) -> p g t", g=2)
        v_sb = carve(41, 4.5, BF16, "p (b f) -> p b f", b=18)
        u_sb = carve(45.5, 8, BF16, "p (b f) -> p b f", b=16)
        sc_sb = carve(53.5, 4, F32, "p (i t) -> p i t", i=2)
        pt_sb = carve(57.5, 6, BF16, "p (i t) -> p i t", i=6)
        stage = carve(53.5, 8, F32, "p (i t) -> p i t", i=2)
        yat = carve(63.5, 8, F32, "p (c t) -> p c t", c=4)
        rden = carve(71.5, 2, F32)
        memT = carve(73.5, 4, BF16, "p (c t) -> p c t", c=8)
        memg = carve(77.5, 4, BF16, "p (c t) -> p c t", c=8)
        km_sb = carve(81.5, 1, BF16, "p (c t) -> p c t", c=2)
        vm_sb = carve(82.5, 1, BF16, "p (c t) -> p c t", c=2)
        sq_sb = carve(83.5, 2, BF16, "p (i t) -> p i t", i=2)
        rstd = carve(85.5, 4, F32, "p (i t) -> p i t", i=2)
        halo_sb = carve(89.5, 1.5, BF16, "p (i t) -> p i t", i=3)
        tab_sb = carve(8, 16, BF16, "p (s b f) -> p s b f", s=2, b=16)
        cusu = carve(24, 2, BF16, "p (s c f) -> p s c f", s=2, c=2)
        yf_sb = carve(26, 16, F32, "p (c t) -> p c t", c=2)
        h2 = carve(8, 32, BF16, "p (c t) -> p c t", c=8)
        hf = carve(40, 8, BF16, "p (c t) -> p c t", c=8)
        relu_t = carve(48, 4, F32, "p (i t) -> p i t", i=2)

        bank = lambda b: ps[:, b * 512:(b + 1) * 512]
        BR = [Res() for _ in range(8)]
        bctr = [0]

        def nb():
            b = bctr[0] % 8; bctr[0] += 1
            return b
        rX = [[Res() for _ in range(4)] for _ in range(8)]
        rWA = Res(); rWB = Res(); rMISC = Res()
        rHQ = [Res() for _ in range(8)]
        rQ = [Res() for _ in range(4)]; rQM = [Res() for _ in range(4)]
        rKD = Res(); rV = Res(); rU = Res()
        rSC = [Res(), Res()]; rPT = [Res() for _ in range(6)]
        rYAT = [Res() for _ in range(4)]; rRDEN = Res()
        rMEMT = Res(); rMEMG = Res(); rKM = Res(); rVM = Res()
        rSQ = [Res(), Res()]; rRSTD = [Res(), Res()]; rHALO = Res()
        rTAB = Res(); rCUSU = Res(); rYF = Res()
        rH2 = [Res() for _ in range(4)]; rHF = Res(); rRELU = [Res(), Res()]
        rSTG = [Res(), Res()]; rSMALL = Res()
        rWD = {}
        rAGI = [Res() for _ in range(L)]; rAGO = [Res() for _ in range(L)]
        rOUT = Res()
        rCONST = Res()
        out_dmas = []

        def ld(dst, src, w, eng="sp", **kw):
            return R.op(eng, lambda h: h.dma_start(out=dst, in_=src), w=w, dma=True, **kw)
        ld(bias_sb[:].rearrange("p a b c -> p (a b c)"), bias_d[:, :], [rCONST])
        ld(ident_f[:], ident_d[:, :], [rCONST]); ld(cs_sb[:].rearrange("p a b -> p (a b)"), cs_d[:, :], [rCONST])
        ld(g_sb[:], gall[:, :], [rCONST]); ld(sink_sb[:], sinks_d[:, :], [rCONST]); ld(oh_sb[:], oh_d[:, :], [rCONST])
        R.op("dve", lambda h: h.tensor_copy(out=ident_b[:], in_=ident_f[:]), r=[rCONST], w=[rMISC])
        R.op("dve", lambda h: h.memset(ones_b[:], 1.0), w=[rMISC])
        R.op("act", lambda h: h.activation(out=sink_sb[:], in_=sink_sb[:], func=AF.Exp), r=[rCONST], w=[rCONST])

        def cast(dst, src, key):
            if os.environ.get('KNOCAST'):
                return
            rWD.setdefault(key, Res())
            R.op("pool", lambda h: h.dma_start(out=dst, in_=src), w=[], dma=True)
            rWD[key].w = R.q["pool"][-1]
        for l in range(L):
            for r0 in range(0, D, 256):
                rs = slice(r0, r0 + 256)
                cast(wb_in[l, rs, 0:512], w_in[l, rs, 0:512], ("in", l))
                for g in range(2):
                    for dup in range(2):
                        cast(wb_in[l, rs, 512 + 128 * g + 64 * dup:512 + 128 * g + 64 * dup + 64], w_in[l, rs, 512 + 64 * g:576 + 64 * g], ("in", l))
                cast(wb_in[l, rs, 768:1408], w_in[l, rs, 640:1280], ("in", l))
                cast(wb_kv[l, rs, :], w_kv[l, rs, :], ("kv", l))
                cast(wb_out[l, rs, :], w_out[l, rs, :], ("out", l))
                cast(wb_f1[l, rs, :], w_f1[l, rs, :], ("f1", l))
            for r0 in range(0, 4096, 256):
                cast(wb_f2[l, r0:r0 + 256, :], w_f2[l, r0:r0 + 256, :], ("f2", l))
        castdone = {}
        for key in rWD:
            pass
        allcasts = [o for o in R.q["pool"] if o.dma]
        R.op("sp", None, extra=allcasts)

        gcol = lambda kind, l, c: (kind * L + l) * 8 + c

        def rms_stats(src_fn, nchunks, nfeat, tile_res_r, ri):
            b = nb()
            for c in range(nchunks):
                si = c % 2
                R.op("act", lambda h, c=c, si=si: h.activation(out=sq_sb[:, si, :], in_=src_fn(c), func=AF.Square),
                     r=tile_res_r(c), w=[rSQ[si]])
                R.op("pe", lambda h, c=c, si=si, b=b: h.matmul(bank(b), lhsT=ones_b[:], rhs=sq_sb[:, si, :], start=(c == 0), stop=(c == nchunks - 1)),
                     r=[rSQ[si], rMISC], w=[BR[b]])
            R.op("act", lambda h, b=b: h.activation(out=rstd[:, ri, :], in_=bank(b), func=AF.Sqrt, bias=EPS, scale=1.0 / nfeat),
                 r=[BR[b]], w=[rRSTD[ri]])
            R.op("dve", lambda h: h.reciprocal(out=rstd[:, ri, :], in_=rstd[:, ri, :]), r=[rRSTD[ri]], w=[rRSTD[ri]])

        def barrier():
            R.barrier()

        for u in range(NU):
            prompt = (u == 0)
            xin = xp if prompt else xs[(u - 1) * S:u * S, :]
            yout = yp if prompt else ys[(u - 1) * S:u * S, :]
            memin = memp if prompt else mems[(u - 1) * 256:u * 256, :]
            barrier()
            R.mark(f'u{u} start')
            for blk in range(16):
                si = blk % 2
                R.op("sp", lambda h, blk=blk, si=si, xin=xin: h.dma_start(out=stage[:, si, :], in_=xin[blk * 128:(blk + 1) * 128, :]),
                     w=[rSTG[si]], dma=True, ov=True)
                for half in range(2):
                    b = nb()
                    def tr(h, blk=blk, si=si, half=half, b=b):
                        for j in range(4):
                            c = half * 4 + j
                            ins = h.transpose(out=bank(b)[:, j * 128:(j + 1) * 128], in_=stage[:, si, c * 128:(c + 1) * 128], identity=ident_f[:])
                        return ins
                    R.op("pe", tr, r=[rSTG[si], rCONST], w=[BR[b]])
                    R.op("dve" if half == 0 else "act",
                         (lambda h, blk=blk, half=half, b=b: h.tensor_copy(out=x_sb[:, half * 4:half * 4 + 4, blk * 128:(blk + 1) * 128], in_=bank(b).rearrange("p (a t) -> p a t", a=4)))
                         if half == 0 else
                         (lambda h, blk=blk, half=half, b=b: h.activation(out=x_sb[:, half * 4:half * 4 + 4, blk * 128:(blk + 1) * 128], in_=bank(b).rearrange("p (a t) -> p a t", a=4), func=AF.Copy)),
                         r=[BR[b]], w=[rX[c][blk // 4] for c in range(half * 4, half * 4 + 4)])
            R.mark(f'u{u} xloaded')
            for mb in range(2):
                R.op("sp", lambda h, mb=mb, memin=memin: h.dma_start(out=stage[:, 0, :], in_=memin[mb * 128:(mb + 1) * 128, :]), w=[rSTG[0]], dma=True, ov=True)
                R.op("act", lambda h: h.activation(out=stage[:, 1, :], in_=stage[:, 0, :], func=AF.Square), r=[rSTG[0]], w=[rSTG[1]])
                R.op("dve", lambda h: h.reduce_sum(out=small[:, 0:1], in_=stage[:, 1, :], axis=AX.X), r=[rSTG[1]], w=[rSMALL])
                R.op("act", lambda h: h.activation(out=small[:, 1:2], in_=small[:, 0:1], func=AF.Sqrt, bias=EPS, scale=1.0 / D), r=[rSMALL], w=[rSMALL])
                R.op("dve", lambda h: h.reciprocal(out=small[:, 2:3], in_=small[:, 1:2]), r=[rSMALL], w=[rSMALL])
                R.op("dve", lambda h: h.tensor_scalar(out=hq[:].rearrange("p c t -> p (c t)")[:, 0:1024], in0=stage[:, 0, :], scalar1=small[:, 2:3], scalar2=None, op0=ALU.mult),
                     r=[rSTG[0], rSMALL], w=rHQ)
                b = nb()
                def trm(h, b=b):
                    bb = bank(b).bitcast(BF16)
                    for c in range(8):
                        ins = h.transpose(out=bb[:, c * 128:(c + 1) * 128], in_=hq[:].rearrange("p c t -> p (c t)")[:, c * 128:(c + 1) * 128], identity=ident_b[:])
                    return ins
                R.op("pe", trm, r=rHQ + [rMISC], w=[BR[b]])
                R.op("dve", lambda h, mb=mb, b=b: h.tensor_copy(out=memT[:, :, mb * 128:(mb + 1) * 128], in_=bank(b).bitcast(BF16).rearrange("p (c t) -> p c t", c=8)),
                     r=[BR[b]], w=[rMEMT])

            R.mark(f'u{u} memdone')
            for l in range(L):
                barrier()
                R.op("sp", lambda h, l=l: h.dma_start(out=wa[:], in_=wb_in[l].rearrange("(c p) n -> p c n", p=128)), w=[rWA], dma=True)
                R.op("sp", lambda h, l=l: h.dma_start(out=wbt[:, :, 0:512], in_=wb_kv[l].rearrange("(c p) n -> p c n", p=128)), w=[rWB], dma=True)
                for c in range(8):
                    R.op("pool", lambda h, c=c, l=l: h.tensor_scalar(out=memg[:, c, :], in0=memT[:, c, :], scalar1=g_sb[:, gcol(1, l, c):gcol(1, l, c) + 1], scalar2=None, op0=ALU.mult),
                         r=[rMEMT, rCONST], w=[rMEMG])
                b = nb()
                def kmm(h, b=b):
                    for cc in range(2):
                        for kc in range(8):
                            ins = h.matmul(bank(b)[:, cc * 256:(cc + 1) * 256], lhsT=wbt[:, kc, cc * 128:(cc + 1) * 128], rhs=memg[:, kc, :], start=(kc == 0), stop=(kc == 7))
                    return ins
                R.op("pe", kmm, r=[rWB, rMEMG], w=[BR[b]])
                R.op("dve", lambda h, b=b: h.tensor_copy(out=km_sb[:].rearrange("p c t -> p (c t)"), in_=bank(b)), r=[BR[b]], w=[rKM])
                b = nb()
                def vmm(h, b=b):
                    for mb in range(2):
                        for kc in range(8):
                            ins = h.matmul(bank(b)[:, mb * 256:(mb + 1) * 256], lhsT=memg[:, kc, mb * 128:(mb + 1) * 128], rhs=wbt[:, kc, 256:512], start=(kc == 0), stop=(kc == 7))
                    return ins
                R.op("pe", vmm, r=[rWB, rMEMG], w=[BR[b]])
                R.op("act", lambda h, b=b: h.activation(out=vm_sb[:].rearrange("p c t -> p (c t)"), in_=bank(b), func=AF.Copy), r=[BR[b]], w=[rVM])

                R.mark(f'u{u} l{l} memkv done')
                for t in range(4):
                    ts = slice(t * 512, (t + 1) * 512)
                    rms_stats(lambda c, ts=ts: x_sb[:, c, ts], 8, D, lambda c, t=t: [rX[c][t]], 0)
                    for c in range(8):
                        R.op("dve", lambda h, c=c, ts=ts, l=l: h.scalar_tensor_tensor(out=hq[:, c, :], in0=x_sb[:, c, ts], scalar=g_sb[:, gcol(0, l, c):gcol(0, l, c) + 1], in1=rstd[:, 0, :], op0=ALU.mult, op1=ALU.mult),
                             r=[rX[c][t], rRSTD[0], rCONST], w=[rHQ[c]])
                    R.mark(f'u{u} l{l} t{t} hq done')
                    outs = [("q", i) for i in range(4)] + [("k", i) for i in range(2)] + [("m", i) for i in range(2)]
                    for kind, i in outs:
                        col = {"q": i * 128, "k": 512 + i * 128, "m": 1152 + i * 128}[kind]
                        b = nb()
                        def mm(h, col=col, b=b):
                            for kc in range(8):
                                ins = h.matmul(bank(b), lhsT=wa[:, kc, col:col + 128], rhs=hq[:, kc, :], start=(kc == 0), stop=(kc == 7))
                            return ins
                        R.op("pe", mm, r=[rWA] + rHQ, w=[BR[b]])
                        if kind == "q":
                            R.op("act", lambda h, i=i, t=t, b=b: h.activation(out=q_sb[:, t, i, :], in_=bank(b), func=AF.Copy, scale=0.125), r=[BR[b]], w=[rQ[t]])
                        elif kind == "m":
                            R.op("act", lambda h, i=i, t=t, b=b: h.activation(out=qm_sb[:, t, i, :], in_=bank(b), func=AF.Copy, scale=0.125), r=[BR[b]], w=[rQM[t]])
                        else:
                            R.op("dve", lambda h, i=i, t=t, b=b: h.tensor_copy(out=kd[:, i, 128 + t * 512:128 + (t + 1) * 512], in_=bank(b)), r=[BR[b]], w=[rKD])
                    R.mark(f'u{u} l{l} t{t} fm done')
                    for j in range(4):
                        blk = t * 4 + j
                        b = nb()
                        def mmv(h, j=j, b=b):
                            for kc in range(8):
                                ins = h.matmul(bank(b)[:, 0:384], lhsT=hq[:, kc, j * 128:(j + 1) * 128], rhs=wa[:, kc, 768:1152], start=(kc == 0), stop=(kc == 7))
                            return ins
                        R.op("pe", mmv, r=[rWA] + rHQ, w=[BR[b]])
                        R.op("dve", lambda h, blk=blk, b=b: h.tensor_copy(out=v_sb[:, 1 + blk, :], in_=bank(b)[:, 0:128]), r=[BR[b]], w=[rV])
                        R.op("dve", lambda h, blk=blk, b=b: h.tensor_copy(out=u_sb[:, blk, :], in_=bank(b)[:, 128:384]), r=[BR[b]], w=[rU])

                R.mark(f'u{u} l{l} inproj done')
                if prompt and not os.environ.get('KNOAG'):
                    R.op("sp", lambda h, l=l: h.dma_start(out=ag_in[l][0:S, :].rearrange("(b p) f -> p b f", p=128), in_=u_sb[:]), r=[rU], w=[rAGI[l]], dma=True, ov=True)
                    R.op("sp", lambda h, l=l: h.dma_start(out=ag_in[l][S:S + 128, :].rearrange("p (g t) -> p g t", g=2), in_=kd[:, :, 128:256]), r=[rKD], w=[rAGI[l]], dma=True, ov=True)
                    R.op("sp", lambda h, l=l: h.dma_start(out=ag_in[l][S + 128:S + 256, :].rearrange("p (g t) -> p g t", g=2), in_=kd[:, :, 128 + 15 * 128:128 + 16 * 128]), r=[rKD], w=[rAGI[l]], dma=True, ov=True)
                    R.op("sp", lambda h, l=l: h.dma_start(out=ag_in[l][S + 256:S + 384, 0:128], in_=v_sb[:, 1, :]), r=[rV], w=[rAGI[l]], dma=True, ov=True)
                    R.op("sp", lambda h, l=l: h.dma_start(out=ag_in[l][S + 256:S + 384, 128:256], in_=v_sb[:, 16, :]), r=[rV], w=[rAGI[l]], dma=True, ov=True)
                    agw = [o for o in R.q["sp"][-5:]]
                    R.op("pool", lambda h, l=l: h.collective_compute("AllGather", ALU.bypass, replica_groups=[list(range(NCORE))], ins=[ag_in[l]], outs=[ag_out[l]]),
                         r=[rAGI[l]], w=[rAGO[l]], csem=csems[l], extra=agw)
                    R.op("dve", lambda h: h.memset(kd[:, :, 0:128], 0.0), w=[rKD])
                    R.op("dve", lambda h: h.memset(kd[:, :, 17 * 128:18 * 128], 0.0), w=[rKD])
                    R.op("dve", lambda h: h.memset(v_sb[:, 0, :], 0.0), w=[rV])
                    R.op("dve", lambda h: h.memset(v_sb[:, 17, :], 0.0), w=[rV])
                    for rk in range(NCORE):
                        R.op("sp", lambda h, l=l, rk=rk: h.dma_start(out=halo_sb[:], in_=ag_out[l][rk * AGR + S:rk * AGR + S + 384, :].rearrange("(i p) f -> p i f", p=128)),
                             r=[rAGO[l]], w=[rHALO], dma=True, ov=True)
                        R.op("dve", lambda h, rk=rk: h.scalar_tensor_tensor(out=kd[:, :, 0:128], in0=halo_sb[:, 1, :].rearrange("p (g t) -> p g t", g=2), scalar=oh_sb[:, rk:rk + 1], in1=kd[:, :, 0:128], op0=ALU.mult, op1=ALU.add),
                             r=[rHALO, rCONST], w=[rKD])
                        R.op("dve", lambda h, rk=rk: h.scalar_tensor_tensor(out=kd[:, :, 17 * 128:18 * 128], in0=halo_sb[:, 0, :].rearrange("p (g t) -> p g t", g=2), scalar=oh_sb[:, 8 + rk:9 + rk], in1=kd[:, :, 17 * 128:18 * 128], op0=ALU.mult, op1=ALU.add),
                             r=[rHALO, rCONST], w=[rKD])
                        R.op("dve", lambda h, rk=rk: h.scalar_tensor_tensor(out=v_sb[:, 0, :], in0=halo_sb[:, 2, 128:256], scalar=oh_sb[:, rk:rk + 1], in1=v_sb[:, 0, :], op0=ALU.mult, op1=ALU.add),
                             r=[rHALO, rCONST], w=[rV])
                        R.op("dve", lambda h, rk=rk: h.scalar_tensor_tensor(out=v_sb[:, 17, :], in0=halo_sb[:, 2, 0:128], scalar=oh_sb[:, 8 + rk:9 + rk], in1=v_sb[:, 17, :], op0=ALU.mult, op1=ALU.add),
                             r=[rHALO, rCONST], w=[rV])

                R.op("sp", lambda h, l=l: h.dma_start(out=wbt[:], in_=wb_out[l].rearrange("(c p) n -> p c n", p=128)), w=[rWB], dma=True)

                R.mark(f'u{u} l{l} ag done')
                for t in range(0 if KSTOP == 'inproj' else 4):
                    ts = slice(t * 512, (t + 1) * 512)
                    for jj in range(4):
                        j = t * 4 + jj
                        kbs = [kb for kb in (j - 1, j, j + 1) if prompt or 0 <= kb < 16]
                        for ki, kb in enumerate(kbs):
                            o = kb - j + 1
                            bl = nb(); bu = nb()
                            def smm(h, kb=kb, jj=jj, t=t, bl=bl, bu=bu):
                                for g in range(2):
                                    h.matmul(bank(bl)[:, g * 256:(g + 1) * 256].rearrange("p (a t) -> p a t", a=2), lhsT=kd[0:64, g, (kb + 1) * 128:(kb + 2) * 128], rhs=q_sb[0:64, t, 2 * g:2 * g + 2, jj * 128:(jj + 1) * 128], start=True, stop=True)
                                    ins = h.matmul(bank(bu)[:, g * 256:(g + 1) * 256].rearrange("p (a t) -> p a t", a=2), lhsT=kd[64:128, g, (kb + 1) * 128:(kb + 2) * 128], rhs=q_sb[64:128, t, 2 * g:2 * g + 2, jj * 128:(jj + 1) * 128], start=True, stop=True)
                                return ins
                            R.op("pe", smm, r=[rKD, rQ[t]], w=[BR[bl], BR[bu]])
                            for hb, bb in ((0, bl), (1, bu)):
                                edge = None
                                if prompt and j == 0 and kb == -1:
                                    edge = 16
                                if prompt and j == 15 and kb == 16:
                                    edge = 17
                                if edge is None:
                                    R.op("dve", lambda h, hb=hb, bb=bb, o=o: h.tensor_tensor(out=sc_sb[:, hb, :], in0=bank(bb), in1=bias_sb[:, o, hb, :], op=ALU.add),
                                         r=[BR[bb], rCONST], w=[rSC[hb]])
                                else:
                                    R.op("dve", lambda h, hb=hb, bb=bb, o=o, edge=edge: h.scalar_tensor_tensor(out=sc_sb[:, hb, :], in0=bank(bb), scalar=oh_sb[:, edge:edge + 1], in1=bias_sb[:, o, hb, :], op0=ALU.add, op1=ALU.add),
                                         r=[BR[bb], rCONST], w=[rSC[hb]])
                                R.op("act", lambda h, hb=hb, ki=ki: h.activation(out=pt_sb[:, hb * 3 + ki, :], in_=sc_sb[:, hb, :], func=AF.Exp),
                                     r=[rSC[hb]], w=[rPT[hb * 3 + ki]])
                        for hb in range(2):
                            bn = nb(); bd = nb()
                            def pv(h, hb=hb, bn=bn, bd=bd, kbs=kbs):
                                for g in range(2):
                                    for ki, kb in enumerate(kbs):
                                        h.matmul(bank(bn)[0:64, g * 256:(g + 1) * 256], lhsT=v_sb[:, kb + 1, g * 64:(g + 1) * 64], rhs=pt_sb[:, hb * 3 + ki, g * 256:(g + 1) * 256], start=(ki == 0), stop=(ki == len(kbs) - 1))
                                for ki, kb in enumerate(kbs):
                                    ins = h.matmul(bank(bd)[0:64, :], lhsT=ones_b[:, 0:64], rhs=pt_sb[:, hb * 3 + ki, :], start=(ki == 0), stop=(ki == len(kbs) - 1))
                                return ins
                            R.op("pe", pv, r=[rV, rMISC] + [rPT[hb * 3 + ki] for ki in range(len(kbs))], w=[BR[bn], BR[bd]])
                            for ge in range(4):
                                hh = 4 * (ge // 2) + 2 * (ge % 2) + hb
                                R.op("dve", lambda h, ge=ge, hh=hh, bd=bd, l=l: h.tensor_scalar(out=rden[0:64, ge * 128:(ge + 1) * 128], in0=bank(bd)[0:64, ge * 128:(ge + 1) * 128], scalar1=sink_sb[0:64, l * 8 + hh:l * 8 + hh + 1], scalar2=None, op0=ALU.add),
                                     r=[BR[bd], rCONST], w=[rRDEN])
                            R.op("dve", lambda h: h.reciprocal(out=rden[0:64, :], in_=rden[0:64, :]), r=[rRDEN], w=[rRDEN])
                            R.op("dve", lambda h, hb=hb, bn=bn, jj=jj: h.tensor_tensor(out=yat[hb * 64:(hb + 1) * 64, :, jj * 128:(jj + 1) * 128], in0=bank(bn)[0:64, :].rearrange("p (c t) -> p c t", c=4), in1=rden[0:64, :].rearrange("p (c t) -> p c t", c=4), op=ALU.mult),
                                 r=[BR[bn], rRDEN], w=rYAT)
                    rms_stats(lambda c: yat[:, c, :], 4, 512, lambda c: [rYAT[c]], 1)
                    for c in range(4):
                        R.op("dve", lambda h, c=c, l=l: h.scalar_tensor_tensor(out=hq[:, c, :], in0=yat[:, c, :], scalar=g_sb[:, gcol(2, l, c):gcol(2, l, c) + 1], in1=rstd[:, 1, :], op0=ALU.mult, op1=ALU.mult),
                             r=[rYAT[c], rRSTD[1], rCONST], w=[rHQ[c]])
                    for mc in range(2):
                        pts = {}
                        for mb in range(2):
                            bl = nb(); bu = nb()
                            def msc(h, mc=mc, mb=mb, t=t, bl=bl, bu=bu):
                                h.matmul(bank(bl), lhsT=km_sb[0:64, mc, mb * 128:(mb + 1) * 128], rhs=qm_sb[0:64, t, mc, :], start=True, stop=True)
                                return h.matmul(bank(bu), lhsT=km_sb[64:128, mc, mb * 128:(mb + 1) * 128], rhs=qm_sb[64:128, t, mc, :], start=True, stop=True)
                            R.op("pe", msc, r=[rKM, rQM[t]], w=[BR[bl], BR[bu]])
                            for hb, bb in ((0, bl), (1, bu)):
                                R.op("act", lambda h, hb=hb, mb=mb, bb=bb: h.activation(out=pt_sb[:, hb * 3 + mb, :], in_=bank(bb), func=AF.Exp), r=[BR[bb]], w=[rPT[hb * 3 + mb]])
                        for hb in range(2):
                            bn = nb(); bd = nb()
                            hm = 2 * mc + hb
                            def mpv(h, hb=hb, hm=hm, bn=bn, bd=bd):
                                for mb in range(2):
                                    h.matmul(bank(bn)[0:64, :], lhsT=vm_sb[:, mb, hm * 64:(hm + 1) * 64], rhs=pt_sb[:, hb * 3 + mb, :], start=(mb == 0), stop=(mb == 1))
                                for mb in range(2):
                                    ins = h.matmul(bank(bd)[0:64, :], lhsT=ones_b[:, 0:64], rhs=pt_sb[:, hb * 3 + mb, :], start=(mb == 0), stop=(mb == 1))
                                return ins
                            R.op("pe", mpv, r=[rVM, rMISC, rPT[hb * 3], rPT[hb * 3 + 1]], w=[BR[bn], BR[bd]])
                            R.op("dve", lambda h, bd=bd: h.reciprocal(out=rden[0:64, :], in_=bank(bd)[0:64, :]), r=[BR[bd]], w=[rRDEN])
                            R.op("dve", lambda h, hb=hb, mc=mc, bn=bn: h.tensor_tensor(out=yat[hb * 64:(hb + 1) * 64, mc, :], in0=bank(bn)[0:64, :], in1=rden[0:64, :], op=ALU.mult),
                                 r=[BR[bn], rRDEN], w=[rYAT[mc]])
                    rms_stats(lambda c: yat[:, c, :], 2, 256, lambda c: [rYAT[c]], 1)
                    for c in range(2):
                        R.op("dve", lambda h, c=c, l=l: h.scalar_tensor_tensor(out=hq[:, 4 + c, :], in0=yat[:, c, :], scalar=g_sb[:, gcol(2, l, 6 + c):gcol(2, l, 6 + c) + 1], in1=rstd[:, 1, :], op0=ALU.mult, op1=ALU.mult),
                             r=[rYAT[c], rRSTD[1], rCONST], w=[rHQ[4 + c]])
                    for oc in range(8):
                        b = nb()
                        def omm(h, oc=oc, b=b):
                            ks = [0, 1, 2, 3, 6, 7]
                            for i, kc in enumerate(ks):
                                ins = h.matmul(bank(b), lhsT=wbt[:, kc, oc * 128:(oc + 1) * 128], rhs=hq[:, i, :], start=(i == 0), stop=(i == 5))
                            return ins
                        R.op("pe", omm, r=[rWB] + rHQ[0:6], w=[BR[b]])
                        R.op("dve", lambda h, oc=oc, ts=ts, b=b: h.tensor_tensor(out=x_sb[:, oc, ts], in0=bank(b), in1=x_sb[:, oc, ts], op=ALU.add), r=[BR[b], rX[oc][t]], w=[rX[oc][t]])

                R.mark(f'u{u} l{l} attn done')
                barrier()
                tabd = tabp_d if prompt else tabs_d
                ngrp = (NPB // 16) if prompt else 1
                for kc in range(0 if KSTOP in ('inproj', 'attn') else 8):
                    bks = [nb() for _ in range(4)]
                    for grp in range(ngrp):
                        if prompt:
                            R.op("sp", lambda h, l=l, grp=grp: h.dma_start(out=u_sb[:], in_=ag_out[l][grp * AGR:grp * AGR + 2048, :].rearrange("(b p) f -> p b f", p=128)),
                                 r=[rAGO[l]], w=[rU], dma=True, ov=True)
                        for sidx in range(2):
                            R.op("sp", lambda h, kc=kc, grp=grp, sidx=sidx: h.dma_start(out=tab_sb[:, sidx, :, :], in_=tabd[sidx, kc, grp * 2048:(grp + 1) * 2048, :].rearrange("(b p) f -> p b f", p=128)),
                                 w=[rTAB], dma=True, ov=True)
                        def dmm(h, grp=grp, bks=bks):
                            for blk in range(16):
                                first = (grp == 0 and blk == 0); last = (grp == ngrp - 1 and blk == 15)
                                for cc in range(2):
                                    h.matmul(bank(bks[cc])[:, 0:256], lhsT=u_sb[:, blk, cc * 128:(cc + 1) * 128], rhs=tab_sb[:, 0, blk, :], start=first, stop=last)
                                    ins = h.matmul(bank(bks[2 + cc])[:, 0:256], lhsT=u_sb[:, blk, cc * 128:(cc + 1) * 128], rhs=tab_sb[:, 1, blk, :], start=first, stop=last)
                            return ins
                        R.op("pe", dmm, r=[rU, rTAB], w=[BR[b] for b in bks])
                    for cc in range(2):
                        R.op("dve", lambda h, cc=cc, bks=bks: h.tensor_copy(out=cusu[:, 0, cc, :], in_=bank(bks[cc])[:, 0:256]), r=[BR[bks[cc]]], w=[rCUSU])
                        R.op("act", lambda h, cc=cc, bks=bks: h.activation(out=cusu[:, 1, cc, :], in_=bank(bks[2 + cc])[:, 0:256], func=AF.Copy), r=[BR[bks[2 + cc]]], w=[rCUSU])
                    b = nb()
                    def cmm(h, b=b):
                        for cc in range(2):
                            h.matmul(bank(b)[:, cc * 256:(cc + 1) * 256], lhsT=cs_sb[:, 0, :], rhs=cusu[:, 0, cc, :], start=True, stop=False)
                            ins = h.matmul(bank(b)[:, cc * 256:(cc + 1) * 256], lhsT=cs_sb[:, 1, :], rhs=cusu[:, 1, cc, :], start=False, stop=True)
                        return ins
                    R.op("pe", cmm, r=[rCUSU, rCONST], w=[BR[b]])
                    R.op("dve", lambda h, kc=kc, b=b: h.tensor_copy(out=yf_sb[:, :, kc * 256:(kc + 1) * 256], in_=bank(b).rearrange("p (c t) -> p c t", c=2)), r=[BR[b]], w=[rYF])
                for t in range(0 if KSTOP in ('inproj', 'attn') else 4):
                    ts = slice(t * 512, (t + 1) * 512)
                    rms_stats(lambda c, ts=ts: yf_sb[:, c, ts], 2, 256, lambda c: [rYF], 1)
                    for c in range(2):
                        R.op("dve", lambda h, c=c, ts=ts, l=l: h.scalar_tensor_tensor(out=hq[:, c, :], in0=yf_sb[:, c, ts], scalar=g_sb[:, gcol(2, l, 4 + c):gcol(2, l, 4 + c) + 1], in1=rstd[:, 1, :], op0=ALU.mult, op1=ALU.mult),
                             r=[rYF, rRSTD[1], rCONST], w=[rHQ[c]])
                    for oc in range(8):
                        b = nb()
                        def omm2(h, oc=oc, b=b):
                            for i in range(2):
                                ins = h.matmul(bank(b), lhsT=wbt[:, 4 + i, oc * 128:(oc + 1) * 128], rhs=hq[:, i, :], start=(i == 0), stop=(i == 1))
                            return ins
                        R.op("pe", omm2, r=[rWB, rHQ[0], rHQ[1]], w=[BR[b]])
                        R.op("dve", lambda h, oc=oc, ts=ts, b=b: h.tensor_tensor(out=x_sb[:, oc, ts], in0=bank(b), in1=x_sb[:, oc, ts], op=ALU.add), r=[BR[b], rX[oc][t]], w=[rX[oc][t]])

                R.mark(f'u{u} l{l} dft done')
                barrier()
                for t in range(4):
                    ts = slice(t * 512, (t + 1) * 512)
                    rms_stats(lambda c, ts=ts: x_sb[:, c, ts], 8, D, lambda c, t=t: [rX[c][t]], 0)
                    for c in range(8):
                        R.op("dve", lambda h, c=c, ts=ts, l=l: h.scalar_tensor_tensor(out=h2[:, c, ts], in0=x_sb[:, c, ts], scalar=g_sb[:, gcol(3, l, c):gcol(3, l, c) + 1], in1=rstd[:, 0, :], op0=ALU.mult, op1=ALU.mult),
                             r=[rX[c][t], rRSTD[0], rCONST], w=[rH2[t]])
                for qd in range(0 if KSTOP in ('inproj', 'attn', 'dft') else 4):
                    R.op("sp", lambda h, l=l, qd=qd: h.dma_start(out=wa[:, :, 0:1024], in_=wb_f1[l][:, qd * 1024:(qd + 1) * 1024].rearrange("(c p) n -> p c n", p=128)), w=[rWA], dma=True)
                    R.op("sp", lambda h, l=l, qd=qd: h.dma_start(out=wbt[:], in_=wb_f2[l][qd * 1024:(qd + 1) * 1024, :].rearrange("(c p) n -> p c n", p=128)), w=[rWB], dma=True)
                    for t in range(4):
                        ts = slice(t * 512, (t + 1) * 512)
                        for fc in range(8):
                            b = nb()
                            def f1(h, fc=fc, ts=ts, b=b):
                                for kc in range(8):
                                    ins = h.matmul(bank(b), lhsT=wa[:, kc, fc * 128:(fc + 1) * 128], rhs=h2[:, kc, ts], start=(kc == 0), stop=(kc == 7))
                                return ins
                            R.op("pe", f1, r=[rWA, rH2[t]], w=[BR[b]])
                            ri = fc % 2
                            R.op("act", lambda h, ri=ri, b=b: h.activation(out=relu_t[:, ri, :], in_=bank(b), func=AF.Relu), r=[BR[b]], w=[rRELU[ri]])
                            R.op("dve", lambda h, ri=ri, fc=fc, b=b: h.tensor_tensor(out=hf[:, fc, :], in0=relu_t[:, ri, :], in1=bank(b), op=ALU.mult), r=[rRELU[ri], BR[b]], w=[rHF])
                        for oc in range(8):
                            b = nb()
                            def f2(h, oc=oc, b=b):
                                for kc in range(8):
                                    ins = h.matmul(bank(b), lhsT=wbt[:, kc, oc * 128:(oc + 1) * 128], rhs=hf[:, kc, :], start=(kc == 0), stop=(kc == 7))
                                return ins
                            R.op("pe", f2, r=[rWB, rHF], w=[BR[b]])
                            R.op("dve", lambda h, oc=oc, ts=ts, b=b: h.tensor_tensor(out=x_sb[:, oc, ts], in0=bank(b), in1=x_sb[:, oc, ts], op=ALU.add), r=[BR[b], rX[oc][t]], w=[rX[oc][t]])

            R.mark(f'u{u} layers done')
            barrier()
            gf = 4 * L * 8
            for t in range(4):
                ts = slice(t * 512, (t + 1) * 512)
                rms_stats(lambda c, ts=ts: x_sb[:, c, ts], 8, D, lambda c, t=t: [rX[c][t]], 0)
                for c in range(8):
                    R.op("dve", lambda h, c=c, ts=ts: h.scalar_tensor_tensor(out=x_sb[:, c, ts], in0=x_sb[:, c, ts], scalar=g_sb[:, gf + c:gf + c + 1], in1=rstd[:, 0, :], op0=ALU.mult, op1=ALU.mult),
                         r=[rX[c][t], rRSTD[0], rCONST], w=[rX[c][t]])
                for jj in range(4):
                    blk = t * 4 + jj
                    si = blk % 2
                    for half in range(2):
                        b = nb()
                        def trb(h, blk=blk, half=half, b=b):
                            for j in range(4):
                                c = half * 4 + j
                                ins = h.transpose(out=bank(b)[:, j * 128:(j + 1) * 128], in_=x_sb[:, c, blk * 128:(blk + 1) * 128], identity=ident_f[:])
                            return ins
                        R.op("pe", trb, r=[rX[c][t] for c in range(half * 4, half * 4 + 4)] + [rCONST], w=[BR[b]])
                        if half == 0:
                            R.op("dve", lambda h, si=si, b=b: h.tensor_copy(out=stage[:, si, 0:512], in_=bank(b)), r=[BR[b]], w=[rSTG[si]])
                        else:
                            R.op("act", lambda h, si=si, b=b: h.activation(out=stage[:, si, 512:1024], in_=bank(b), func=AF.Copy), r=[BR[b]], w=[rSTG[si]])
                    out_dmas.append(R.op("sp", lambda h, blk=blk, si=si, yout=yout: h.dma_start(out=yout[blk * 128:(blk + 1) * 128, :], in_=stage[:, si, :]), r=[rSTG[si]], w=[], dma=True, ov=True))
        barrier()
        R.op("sp", None, extra=out_dmas)

        for e in ENGS:
            c = 0
            for o in R.q[e]:
                if (not o.dma) and o.sig:
                    c += 1; o.cnt = c

        def semof(d):
            if d.csem is not None:
                return d.csem
            if d.dma:
                return dsem[d.semidx]
            return esem[d.eng]

        def run(e, h):
            for o in R.q[e]:
                for d in o.deps:
                    h.wait_ge(semof(d), d.cnt)
                if o.fn is None:
                    if o.sig:
                        h.nop().then_inc(esem[e], 1)
                    continue
                ins = o.fn(h)
                if o.csem is not None:
                    ins.then_inc(o.csem)
                elif o.dma:
                    ins.then_inc(dsem[o.semidx], 16)
                elif o.sig:
                    ins.then_inc(esem[e], 1)

        @block.tensor
        def _(h):
            run("pe", h)

        @block.scalar
        def _(h):
            run("act", h)

        @block.vector
        def _(h):
            run("dve", h)

        @block.gpsimd
        def _(h):
            run("pool", h)

        @block.sync
        def _(h):
            run("sp", h)
    return nc


def prep_inputs(L, NS, n_prompt_tok, x_prompt, x_sample, mem_prompt, mem_sample, g_mix, w_in, g_mem, w_mem_kv, sinks,
                g_grp, w_out, g_ffn, w_ff1, w_ff2, g_final):
    bias, cs, tab_s, tab_p = host_tables(n_prompt_tok)
    f = lambda a: np.ascontiguousarray(np.asarray(a, dtype=np.float32))
    NG = 4 * L * 8 + 8
    gall = np.zeros((128, NG), np.float32)
    for kind, g in enumerate([g_mix, g_mem, g_grp, g_ffn]):
        g = f(g)
        for l in range(L):
            gall[:, (kind * L + l) * 8:(kind * L + l) * 8 + 8] = g[l].reshape(8, 128).T
    gall[:, 4 * L * 8:] = f(g_final).reshape(8, 128).T
    LW = max(L, 1)
    sk = np.broadcast_to(f(sinks)[:LW].reshape(1, LW * 8), (128, LW * 8)).copy()
    xp = f(x_prompt).reshape(-1, D); xsm = f(x_sample); memp = f(mem_prompt).reshape(-1, D); memsm = f(mem_sample)
    maps = []
    for c in range(NCORE):
        oh = np.zeros((128, 18), np.float32)
        if c > 0:
            oh[:, c - 1] = 1.0
        else:
            oh[:, 16] = NEG
        if c < NCORE - 1:
            oh[:, 8 + c + 1] = 1.0
        else:
            oh[:, 17] = NEG
        maps.append({
            "xp": xp[c * S:(c + 1) * S], "xs": xsm[c * NS:(c + 1) * NS].reshape(NS * S, D),
            "memp": memp, "mems": memsm[c * NS:(c + 1) * NS].reshape(NS * 256, D),
            "w_in": f(w_in)[:LW], "w_kv": f(w_mem_kv)[:LW], "w_out": f(w_out)[:LW], "w_f1": f(w_ff1)[:LW], "w_f2": f(w_ff2)[:LW],
            "gall": gall, "sinks": sk, "oh": oh, "bias": bias, "ident": np.eye(128, dtype=np.float32),
            "cs64": cs, "tab_s": tab_s, "tab_p": tab_p[c],
        })
    return maps


def run(L, NS, inputs):
    n_prompt_tok = inputs["x_prompt"].shape[1]
    nc = build(L, NS, n_prompt_tok // 128)
    maps = prep_inputs(L, NS, n_prompt_tok, **inputs)
    res = run_bass_kernel_spmd(nc, maps, core_ids=list(range(NCORE)))
    yp = np.concatenate([r["yp"] for r in res.results], axis=0)[None]
    ys = np.concatenate([r["ys"].reshape(NS, S, D) for r in res.results], axis=0)
    return yp.astype(np.float32), ys.astype(np.float32)


def kernel(**inputs):
    return run(4, 4, inputs)
```

```python
import os
import types
import numpy as np
import ml_dtypes
import concourse.bass as bass
import concourse.mybir as mybir
from concourse.bass_utils import run_bass_kernel_spmd

F32 = mybir.dt.float32
BF16 = mybir.dt.bfloat16
ALU = mybir.AluOpType
AF = mybir.ActivationFunctionType
AX = mybir.AxisListType
NPBF = ml_dtypes.bfloat16

D = 1024
S = 2048
NCORE = 8
EPS = 1e-6
NEG = -30000.0
HALO = 384
AGR = S + HALO
ENGS = ["pe", "act", "dve", "pool", "sp"]
NDSEM = 24
KSTOP = os.environ.get('KSTOP', '')
KCUT = int(os.environ.get('KCUT', '0'))
KLOG = os.environ.get('KLOG', '')


class Res:
    __slots__ = ("w", "rs")

    def __init__(self):
        self.w = None
        self.rs = []


class Op:
    __slots__ = ("eng", "idx", "fn", "deps", "dma", "sig", "cnt", "known", "semidx", "csem", "ov")


class Rec:
    def __init__(self):
        self.q = {e: [] for e in ENGS}
        self.known = {e: {f: -1 for f in ENGS} for e in ENGS}
        self.wd = {e: set() for e in ENGS}
        self.dmas = []
        self.ovd = []

    def mark(self, name):
        if KLOG:
            print("MARK", name, getattr(self, "nops", 0), flush=True)

    def op(self, eng, fn, r=(), w=(), dma=False, csem=None, ov=False, extra=()):
        self.nops = getattr(self, "nops", 0) + 1
        if KCUT and self.nops > KCUT:
            return None
        extra = [e_ for e_ in extra if e_ is not None]
        if fn is not None and fn.__closure__:
            cells = []
            for c in fn.__closure__:
                try:
                    cells.append(types.CellType(c.cell_contents))
                except ValueError:
                    cells.append(c)
            f2 = types.FunctionType(fn.__code__, fn.__globals__, fn.__name__, fn.__defaults__, tuple(cells))
            f2.__kwdefaults__ = fn.__kwdefaults__
            fn = f2
        o = Op()
        o.eng = eng; o.idx = len(self.q[eng]); o.fn = fn; o.dma = dma or (csem is not None)
        o.sig = False; o.deps = []; o.cnt = 0; o.csem = csem; o.semidx = -1; o.ov = ov
        cand = list(extra)
        for x in r:
            if x.w is not None:
                cand.append(x.w)
        for x in w:
            if x.w is not None:
                cand.append(x.w)
            cand.extend(x.rs)
        if csem is not None:
            o.cnt = 1; o.sig = True
        elif dma:
            n = len(self.dmas)
            o.semidx = n % NDSEM; o.cnt = 16 * (n // NDSEM + 1)
            if n >= NDSEM:
                cand.append(self.dmas[n - NDSEM])
            self.dmas.append(o); o.sig = True
        if o.dma and ov:
            self.ovd.append(o)
        kn = self.known[eng]
        best = {}
        for d in cand:
            if d is o:
                continue
            if d.dma:
                if id(d) not in self.wd[eng]:
                    self.wd[eng].add(id(d)); o.deps.append(d)
                    for f, v in d.known.items():
                        if v > kn[f]:
                            kn[f] = v
            else:
                if d.eng == eng and eng in ("pe", "sp"):
                    continue
                if kn[d.eng] >= d.idx:
                    continue
                if d.eng not in best or best[d.eng].idx < d.idx:
                    best[d.eng] = d
        for d in best.values():
            if kn[d.eng] >= d.idx:
                continue
            d.sig = True; o.deps.append(d)
            kn[d.eng] = d.idx
            for f, v in d.known.items():
                if v > kn[f]:
                    kn[f] = v
        o.known = dict(kn)
        self.q[eng].append(o)
        for x in w:
            x.w = o; x.rs = []
        for x in r:
            if x.w is not o:
                x.rs.append(o)
        return o

    def barrier(self):
        ex = []
        for f in ENGS:
            for o in reversed(self.q[f]):
                if (not o.dma) and o.fn is not None:
                    ex.append(o); break
        pend = list(self.ovd); self.ovd = []
        for e in ENGS:
            self.op(e, None, extra=ex + pend)


def _bf(a):
    return np.ascontiguousarray(a.astype(NPBF))


def host_tables(n_prompt_tok):
    H = 8
    slopes = np.exp2(-8.0 * np.arange(1, H + 1, dtype=np.float64) / H)
    key = np.arange(128)[:, None]; q = np.arange(128)[None, :]
    bias = np.zeros((128, 3, 2, 512), np.float32)
    for o in range(3):
        rel = key + (o - 1) * 128 - q
        ok = np.abs(rel) <= 128
        for hb in range(2):
            for g in range(2):
                for e in range(2):
                    h = 4 * g + 2 * e + hb
                    b = np.where(ok, -slopes[h] * np.abs(rel), NEG)
                    bias[:, o, hb, (g * 2 + e) * 128:(g * 2 + e + 1) * 128] = b
    bias = bias.reshape(128, 3 * 2 * 512)
    c = np.arange(64)
    ang = 2 * np.pi * np.outer(c, c) / 64.0
    C64 = np.cos(ang) / 8.0; S64 = np.sin(ang) / 8.0
    cs = np.zeros((128, 2, 128), np.float64)
    for gg in range(2):
        cs[gg * 64:(gg + 1) * 64, 0, gg * 64:(gg + 1) * 64] = C64
        cs[gg * 64:(gg + 1) * 64, 1, gg * 64:(gg + 1) * 64] = -S64
    cs = _bf(cs.reshape(128, 256))

    def dtab(ntot, k0):
        n = np.arange(ntot, dtype=np.int64)[:, None]
        out = np.empty((2, 8, ntot, 256), NPBF)
        for kc in range(8):
            k = (k0 + kc * 256 + np.arange(256, dtype=np.int64))[None, :]
            a = 2 * np.pi * ((n * k) % ntot).astype(np.float64) / ntot
            out[0, kc] = (np.cos(a) / np.sqrt(ntot)).astype(NPBF)
            out[1, kc] = (np.sin(a) / np.sqrt(ntot)).astype(NPBF)
        ng = ntot // 2048
        out = out.reshape(2, 8, ng, 16, 128, 256).transpose(0, 1, 2, 4, 3, 5)
        return np.ascontiguousarray(out).reshape(2, 8, ng * 128, 4096)
    tab_s = dtab(S, 0)
    tab_p = [dtab(n_prompt_tok, c_ * S) for c_ in range(NCORE)]
    return bias, cs, tab_s, tab_p


def build(L, NS, NPB):
    nc = bass.Bass("TRN2", target_bir_lowering=False)
    R = Rec()
    NU = 1 + NS
    di = lambda n, s, dt=F32: nc.dram_tensor(n, s, dt, kind="ExternalInput").ap()
    xp = di("xp", [S, D]); xs = di("xs", [NS * S, D]); memp = di("memp", [256, D]); mems = di("mems", [NS * 256, D])
    LW = max(L, 1)
    w_in = di("w_in", [LW, D, 1280]); w_kv = di("w_kv", [LW, D, 512]); w_out = di("w_out", [LW, D, D])
    w_f1 = di("w_f1", [LW, D, 4096]); w_f2 = di("w_f2", [LW, 4096, D])
    NG = 4 * L * 8 + 8
    gall = di("gall", [128, NG]); sinks_d = di("sinks", [128, LW * 8]); oh_d = di("oh", [128, 18])
    bias_d = di("bias", [128, 3072]); ident_d = di("ident", [128, 128])
    cs_d = di("cs64", [128, 256], BF16)
    tabs_d = di("tab_s", [2, 8, 128, 4096], BF16); tabp_d = di("tab_p", [2, 8, NPB * 8, 4096], BF16)
    yp = nc.dram_tensor("yp", [S, D], F32, kind="ExternalOutput").ap()
    ys = nc.dram_tensor("ys", [NS * S, D], F32, kind="ExternalOutput").ap()
    dn = lambda n, s: nc.dram_tensor(n, s, BF16, kind="Internal").ap()
    wb_in = dn("wb_in", [LW, D, 1408]); wb_kv = dn("wb_kv", [LW, D, 512]); wb_out = dn("wb_out", [LW, D, D])
    wb_f1 = dn("wb_f1", [LW, D, 4096]); wb_f2 = dn("wb_f2", [LW, 4096, D])
    ag_in = [dn(f"ag_in{l}", [AGR, 256]) for l in range(L)]
    ag_out = [dn(f"ag_out{l}", [NCORE * AGR, 256]) for l in range(L)]

    import contextlib
    es = contextlib.ExitStack()
    with es:
        sb = lambda n, s, dt: es.enter_context(nc.sbuf_tensor(n, s, dt))
        x_sb = sb("x_sb", [128, 8, S], F32)
        wa = sb("wa", [128, 8, 1408], BF16)
        wbt = sb("wbt", [128, 8, 1024], BF16)
        bias_sb = sb("bias_sb", [128, 3, 2, 512], F32)
        ident_f = sb("ident_f", [128, 128], F32)
        ident_b = sb("ident_b", [128, 128], BF16)
        ones_b = sb("ones_b", [128, 128], BF16)
        cs_sb = sb("cs_sb", [128, 2, 128], BF16)
        g_sb = sb("g_sb", [128, NG], F32)
        sink_sb = sb("sink_sb", [128, LW * 8], F32)
        oh_sb = sb("oh_sb", [128, 18], F32)
        small = sb("small", [128, 8], F32)
        OVN = 91 * 512
        ov = sb("ov", [128, OVN], BF16)
        ps = es.enter_context(nc.psum_tensor("ps", [128, 4096], F32))
        esem = {e: es.enter_context(nc.semaphore("s_" + e)) for e in ENGS}
        dsem = [es.enter_context(nc.semaphore(f"d{i}")) for i in range(NDSEM)]
        csems = [es.enter_context(nc.semaphore(f"cc{l}")) for l in range(L)]
        block = es.enter_context(nc.Block())

        K = 512

        def carve(off_k, nbytes_k, dt, pat=None, **kw):
            a = ov[:, int(off_k * K):int((off_k + nbytes_k) * K)]
            if dt == F32:
                a = a.bitcast(F32)
            if pat:
                a = a.rearrange(pat, **kw)
            return a
        hq = carve(0, 8, BF16, "p (c t) -> p c t", c=8)
        q_sb = carve(8, 16, BF16, "p (i c t) -> p i c t", i=4, c=4)
        qm_sb = carve(24, 8, BF16, "p (i c t) -> p i c t", i=4, c=2)
        kd = carve(32, 9, BF16, "p (g t) -> p g t", g=2)
        v_sb = carve(41, 4.5, BF16, "p (b f) -> p b f", b=18)
        u_sb = carve(45.5, 8, BF16, "p (b f) -> p b f", b=16)
        sc_sb = carve(53.5, 4, F32, "p (i t) -> p i t", i=2)
        pt_sb = carve(57.5, 6, BF16, "p (i t) -> p i t", i=6)
        stage = carve(53.5, 8, F32, "p (i t) -> p i t", i=2)
        yat = carve(63.5, 8, F32, "p (c t) -> p c t", c=4)
        rden = carve(71.5, 2, F32)
        memT = carve(73.5, 4, BF16, "p (c t) -> p c t", c=8)
        memg = carve(77.5, 4, BF16, "p (c t) -> p c t", c=8)
        km_sb = carve(81.5, 1, BF16, "p (c t) -> p c t", c=2)
        vm_sb = carve(82.5, 1, BF16, "p (c t) -> p c t", c=2)
        sq_sb = carve(83.5, 2, BF16, "p (i t) -> p i t", i=2)
        rstd = carve(85.5, 4, F32, "p (i t) -> p i t", i=2)
        halo_sb = carve(89.5, 1.5, BF16, "p (i t) -> p i t", i=3)
        tab_sb = carve(8, 16, BF16, "p (s b f) -> p s b f", s=2, b=16)
        cusu = carve(24, 2, BF16, "p (s c f) -> p s c f", s=2, c=2)
        yf_sb = carve(26, 16, F32, "p (c t) -> p c t", c=2)
        h2 = carve(8, 32, BF16, "p (c t) -> p c t", c=8)
        hf = carve(40, 8, BF16, "p (c t) -> p c t", c=8)
        relu_t = carve(48, 4, F32, "p (i t) -> p i t", i=2)

        bank = lambda b: ps[:, b * 512:(b + 1) * 512]
        BR = [Res() for _ in range(8)]
        bctr = [0]

        def nb():
            b = bctr[0] % 8; bctr[0] += 1
            return b
        rX = [[Res() for _ in range(4)] for _ in range(8)]
        rWA = Res(); rWB = Res(); rMISC = Res()
        rHQ = [Res() for _ in range(8)]
        rQ = [Res() for _ in range(4)]; rQM = [Res() for _ in range(4)]
        rKD = Res(); rV = Res(); rU = Res()
        rSC = [Res(), Res()]; rPT = [Res() for _ in range(6)]
        rYAT = [Res() for _ in range(4)]; rRDEN = Res()
        rMEMT = Res(); rMEMG = Res(); rKM = Res(); rVM = Res()
        rSQ = [Res(), Res()]; rRSTD = [Res(), Res()]; rHALO = Res()
        rTAB = Res(); rCUSU = Res(); rYF = Res()
        rTABH = [Res(), Res()]; rUH = [Res(), Res()]; dstep = [0]
        rH2 = [Res() for _ in range(4)]; rHF = Res(); rRELU = [Res(), Res()]
        rSTG = [Res(), Res()]; rSMALL = Res()
        rWD = {}
        rAGI = [Res() for _ in range(L)]; rAGO = [Res() for _ in range(L)]
        rOUT = Res()
        rCONST = Res()
        out_dmas = []

        def ld(dst, src, w, eng="sp", **kw):
            return R.op(eng, lambda h: h.dma_start(out=dst, in_=src), w=w, dma=True, **kw)
        ld(bias_sb[:].rearrange("p a b c -> p (a b c)"), bias_d[:, :], [rCONST])
        ld(ident_f[:], ident_d[:, :], [rCONST]); ld(cs_sb[:].rearrange("p a b -> p (a b)"), cs_d[:, :], [rCONST])
        ld(g_sb[:], gall[:, :], [rCONST]); ld(sink_sb[:], sinks_d[:, :], [rCONST]); ld(oh_sb[:], oh_d[:, :], [rCONST])
        R.op("dve", lambda h: h.tensor_copy(out=ident_b[:], in_=ident_f[:]), r=[rCONST], w=[rMISC])
        R.op("dve", lambda h: h.memset(ones_b[:], 1.0), w=[rMISC])
        R.op("act", lambda h: h.activation(out=sink_sb[:], in_=sink_sb[:], func=AF.Exp), r=[rCONST], w=[rCONST])

        castops = {}

        def cast(dst, src, key):
            castops.setdefault(key, [])
            if os.environ.get('KNOCAST'):
                return
            castops[key].append(R.op("pool", lambda h: h.dma_start(out=dst, in_=src), w=[], dma=True))

        def cast_layer(l):
            for r0 in range(0, D, 256):
                rs = slice(r0, r0 + 256)
                cast(wb_in[l, rs, 0:512], w_in[l, rs, 0:512], ("in", l))
                for g in range(2):
                    for dup in range(2):
                        cast(wb_in[l, rs, 512 + 128 * g + 64 * dup:512 + 128 * g + 64 * dup + 64], w_in[l, rs, 512 + 64 * g:576 + 64 * g], ("in", l))
                cast(wb_in[l, rs, 768:1408], w_in[l, rs, 640:1280], ("in", l))
            for r0 in range(0, D, 256):
                cast(wb_kv[l, r0:r0 + 256, :], w_kv[l, r0:r0 + 256, :], ("kv", l))
            for r0 in range(0, D, 256):
                cast(wb_out[l, r0:r0 + 256, :], w_out[l, r0:r0 + 256, :], ("out", l))
            for r0 in range(0, D, 256):
                cast(wb_f1[l, r0:r0 + 256, :], w_f1[l, r0:r0 + 256, :], ("f1", l))
            for r0 in range(0, 4096, 256):
                cast(wb_f2[l, r0:r0 + 256, :], w_f2[l, r0:r0 + 256, :], ("f2", l))
        if L > 0:
            cast_layer(0)

        gcol = lambda kind, l, c: (kind * L + l) * 8 + c

        def rms_stats(src_fn, nchunks, nfeat, tile_res_r, ri):
            b = nb()
            for c in range(nchunks):
                si = c % 2
                R.op("act", lambda h, c=c, si=si: h.activation(out=sq_sb[:, si, :], in_=src_fn(c), func=AF.Square),
                     r=tile_res_r(c), w=[rSQ[si]])
                R.op("pe", lambda h, c=c, si=si, b=b: h.matmul(bank(b), lhsT=ones_b[:], rhs=sq_sb[:, si, :], start=(c == 0), stop=(c == nchunks - 1)),
                     r=[rSQ[si], rMISC], w=[BR[b]])
            R.op("act", lambda h, b=b: h.activation(out=rstd[:, ri, :], in_=bank(b), func=AF.Ln, bias=EPS, scale=1.0 / nfeat),
                 r=[BR[b]], w=[rRSTD[ri]])
            R.op("act", lambda h: h.activation(out=rstd[:, ri, :], in_=rstd[:, ri, :], func=AF.Exp, scale=-0.5), r=[rRSTD[ri]], w=[rRSTD[ri]])

        def barrier():
            R.barrier()

        for u in range(NU):
            prompt = (u == 0)
            xin = xp if prompt else xs[(u - 1) * S:u * S, :]
            yout = yp if prompt else ys[(u - 1) * S:u * S, :]
            memin = memp if prompt else mems[(u - 1) * 256:u * 256, :]
            barrier()
            R.mark(f'u{u} start')
            for blk in range(16):
                si = blk % 2
                R.op("sp", lambda h, blk=blk, si=si, xin=xin: h.dma_start(out=stage[:, si, :], in_=xin[blk * 128:(blk + 1) * 128, :]),
                     w=[rSTG[si]], dma=True, ov=True)
                for half in range(2):
                    b = nb()
                    def tr(h, blk=blk, si=si, half=half, b=b):
                        for j in range(4):
                            c = half * 4 + j
                            ins = h.transpose(out=bank(b)[:, j * 128:(j + 1) * 128], in_=stage[:, si, c * 128:(c + 1) * 128], identity=ident_f[:])
                        return ins
                    R.op("pe", tr, r=[rSTG[si], rCONST], w=[BR[b]])
                    R.op("dve" if half == 0 else "act",
                         (lambda h, blk=blk, half=half, b=b: h.tensor_copy(out=x_sb[:, half * 4:half * 4 + 4, blk * 128:(blk + 1) * 128], in_=bank(b).rearrange("p (a t) -> p a t", a=4)))
                         if half == 0 else
                         (lambda h, blk=blk, half=half, b=b: h.activation(out=x_sb[:, half * 4:half * 4 + 4, blk * 128:(blk + 1) * 128], in_=bank(b).rearrange("p (a t) -> p a t", a=4), func=AF.Copy)),
                         r=[BR[b]], w=[rX[c][blk // 4] for c in range(half * 4, half * 4 + 4)])
            R.mark(f'u{u} xloaded')
            for mb in range(2):
                R.op("sp", lambda h, mb=mb, memin=memin: h.dma_start(out=stage[:, 0, :], in_=memin[mb * 128:(mb + 1) * 128, :]), w=[rSTG[0]], dma=True, ov=True)
                R.op("act", lambda h: h.activation(out=stage[:, 1, :], in_=stage[:, 0, :], func=AF.Square), r=[rSTG[0]], w=[rSTG[1]])
                R.op("dve", lambda h: h.reduce_sum(out=small[:, 0:1], in_=stage[:, 1, :], axis=AX.X), r=[rSTG[1]], w=[rSMALL])
                R.op("act", lambda h: h.activation(out=small[:, 1:2], in_=small[:, 0:1], func=AF.Sqrt, bias=EPS, scale=1.0 / D), r=[rSMALL], w=[rSMALL])
                R.op("dve", lambda h: h.reciprocal(out=small[:, 2:3], in_=small[:, 1:2]), r=[rSMALL], w=[rSMALL])
                R.op("dve", lambda h: h.tensor_scalar(out=hq[:].rearrange("p c t -> p (c t)")[:, 0:1024], in0=stage[:, 0, :], scalar1=small[:, 2:3], scalar2=None, op0=ALU.mult),
                     r=[rSTG[0], rSMALL], w=rHQ)
                b = nb()
                def trm(h, b=b):
                    bb = bank(b).bitcast(BF16)
                    for c in range(8):
                        ins = h.transpose(out=bb[:, c * 128:(c + 1) * 128], in_=hq[:].rearrange("p c t -> p (c t)")[:, c * 128:(c + 1) * 128], identity=ident_b[:])
                    return ins
                R.op("pe", trm, r=rHQ + [rMISC], w=[BR[b]])
                R.op("dve", lambda h, mb=mb, b=b: h.tensor_copy(out=memT[:, :, mb * 128:(mb + 1) * 128], in_=bank(b).bitcast(BF16).rearrange("p (c t) -> p c t", c=8)),
                     r=[BR[b]], w=[rMEMT])

            R.mark(f'u{u} memdone')
            for l in range(L):
                barrier()
                R.op("sp", lambda h, l=l: h.dma_start(out=wa[:], in_=wb_in[l].rearrange("(c p) n -> p c n", p=128)), w=[rWA], dma=True, extra=castops[("in", l)])
                R.op("sp", lambda h, l=l: h.dma_start(out=wbt[:, :, 0:512], in_=wb_kv[l].rearrange("(c p) n -> p c n", p=128)), w=[rWB], dma=True, extra=castops[("kv", l)])
                for c in range(8):
                    R.op("dve", lambda h, c=c, l=l: h.tensor_scalar(out=memg[:, c, :], in0=memT[:, c, :], scalar1=g_sb[:, gcol(1, l, c):gcol(1, l, c) + 1], scalar2=None, op0=ALU.mult),
                         r=[rMEMT, rCONST], w=[rMEMG])
                b = nb()
                def kmm(h, b=b):
                    for cc in range(2):
                        for kc in range(8):
                            ins = h.matmul(bank(b)[:, cc * 256:(cc + 1) * 256], lhsT=wbt[:, kc, cc * 128:(cc + 1) * 128], rhs=memg[:, kc, :], start=(kc == 0), stop=(kc == 7))
                    return ins
                R.op("pe", kmm, r=[rWB, rMEMG], w=[BR[b]])
                R.op("dve", lambda h, b=b: h.tensor_copy(out=km_sb[:].rearrange("p c t -> p (c t)"), in_=bank(b)), r=[BR[b]], w=[rKM])
                b = nb()
                def vmm(h, b=b):
                    for mb in range(2):
                        for kc in range(8):
                            ins = h.matmul(bank(b)[:, mb * 256:(mb + 1) * 256], lhsT=memg[:, kc, mb * 128:(mb + 1) * 128], rhs=wbt[:, kc, 256:512], start=(kc == 0), stop=(kc == 7))
                    return ins
                R.op("pe", vmm, r=[rWB, rMEMG], w=[BR[b]])
                R.op("act", lambda h, b=b: h.activation(out=vm_sb[:].rearrange("p c t -> p (c t)"), in_=bank(b), func=AF.Copy), r=[BR[b]], w=[rVM])

                R.mark(f'u{u} l{l} memkv done')
                for t in range(4):
                    ts = slice(t * 512, (t + 1) * 512)
                    rms_stats(lambda c, ts=ts: x_sb[:, c, ts], 8, D, lambda c, t=t: [rX[c][t]], 0)
                    for c in range(8):
                        R.op("dve", lambda h, c=c, ts=ts, l=l: h.scalar_tensor_tensor(out=hq[:, c, :], in0=x_sb[:, c, ts], scalar=g_sb[:, gcol(0, l, c):gcol(0, l, c) + 1], in1=rstd[:, 0, :], op0=ALU.mult, op1=ALU.mult),
                             r=[rX[c][t], rRSTD[0], rCONST], w=[rHQ[c]])
                    R.mark(f'u{u} l{l} t{t} hq done')
                    outs = [("q", i) for i in range(4)] + [("k", i) for i in range(2)] + [("m", i) for i in range(2)]
                    for kind, i in outs:
                        col = {"q": i * 128, "k": 512 + i * 128, "m": 1152 + i * 128}[kind]
                        b = nb()
                        def mm(h, col=col, b=b):
                            for kc in range(8):
                                ins = h.matmul(bank(b), lhsT=wa[:, kc, col:col + 128], rhs=hq[:, kc, :], start=(kc == 0), stop=(kc == 7))
                            return ins
                        R.op("pe", mm, r=[rWA] + rHQ, w=[BR[b]])
                        if kind == "q":
                            R.op("act", lambda h, i=i, t=t, b=b: h.activation(out=q_sb[:, t, i, :], in_=bank(b), func=AF.Copy, scale=0.125), r=[BR[b]], w=[rQ[t]])
                        elif kind == "m":
                            R.op("act", lambda h, i=i, t=t, b=b: h.activation(out=qm_sb[:, t, i, :], in_=bank(b), func=AF.Copy, scale=0.125), r=[BR[b]], w=[rQM[t]])
                        else:
                            R.op("dve", lambda h, i=i, t=t, b=b: h.tensor_copy(out=kd[:, i, 128 + t * 512:128 + (t + 1) * 512], in_=bank(b)), r=[BR[b]], w=[rKD])
                    R.mark(f'u{u} l{l} t{t} fm done')
                    for j in range(4):
                        blk = t * 4 + j
                        b = nb()
                        def mmv(h, j=j, b=b):
                            for kc in range(8):
                                ins = h.matmul(bank(b)[:, 0:384], lhsT=hq[:, kc, j * 128:(j + 1) * 128], rhs=wa[:, kc, 768:1152], start=(kc == 0), stop=(kc == 7))
                            return ins
                        R.op("pe", mmv, r=[rWA] + rHQ, w=[BR[b]])
                        R.op("dve", lambda h, blk=blk, b=b: h.tensor_copy(out=v_sb[:, 1 + blk, :], in_=bank(b)[:, 0:128]), r=[BR[b]], w=[rV])
                        R.op("dve", lambda h, blk=blk, b=b: h.tensor_copy(out=u_sb[:, blk, :], in_=bank(b)[:, 128:384]), r=[BR[b]], w=[rU])

                R.mark(f'u{u} l{l} inproj done')
                if prompt and not os.environ.get('KNOAG'):
                    R.op("sp", lambda h, l=l: h.dma_start(out=ag_in[l][0:S, :].rearrange("(b p) f -> p b f", p=128), in_=u_sb[:]), r=[rU], w=[rAGI[l]], dma=True, ov=True)
                    R.op("sp", lambda h, l=l: h.dma_start(out=ag_in[l][S:S + 128, :].rearrange("p (g t) -> p g t", g=2), in_=kd[:, :, 128:256]), r=[rKD], w=[rAGI[l]], dma=True, ov=True)
                    R.op("sp", lambda h, l=l: h.dma_start(out=ag_in[l][S + 128:S + 256, :].rearrange("p (g t) -> p g t", g=2), in_=kd[:, :, 128 + 15 * 128:128 + 16 * 128]), r=[rKD], w=[rAGI[l]], dma=True, ov=True)
                    R.op("sp", lambda h, l=l: h.dma_start(out=ag_in[l][S + 256:S + 384, 0:128], in_=v_sb[:, 1, :]), r=[rV], w=[rAGI[l]], dma=True, ov=True)
                    R.op("sp", lambda h, l=l: h.dma_start(out=ag_in[l][S + 256:S + 384, 128:256], in_=v_sb[:, 16, :]), r=[rV], w=[rAGI[l]], dma=True, ov=True)
                    agw = [o for o in R.q["sp"][-5:]]
                    R.op("pool", lambda h, l=l: h.collective_compute("AllGather", ALU.bypass, replica_groups=[list(range(NCORE))], ins=[ag_in[l]], outs=[ag_out[l]]),
                         r=[rAGI[l]], w=[rAGO[l]], csem=csems[l], extra=agw)
                    R.op("dve", lambda h: h.memset(kd[:, :, 0:128], 0.0), w=[rKD])
                    R.op("dve", lambda h: h.memset(kd[:, :, 17 * 128:18 * 128], 0.0), w=[rKD])
                    R.op("dve", lambda h: h.memset(v_sb[:, 0, :], 0.0), w=[rV])
                    R.op("dve", lambda h: h.memset(v_sb[:, 17, :], 0.0), w=[rV])
                    for rk in range(NCORE):
                        R.op("sp", lambda h, l=l, rk=rk: h.dma_start(out=halo_sb[:], in_=ag_out[l][rk * AGR + S:rk * AGR + S + 384, :].rearrange("(i p) f -> p i f", p=128)),
                             r=[rAGO[l]], w=[rHALO], dma=True, ov=True)
                        R.op("dve", lambda h, rk=rk: h.scalar_tensor_tensor(out=kd[:, :, 0:128], in0=halo_sb[:, 1, :].rearrange("p (g t) -> p g t", g=2), scalar=oh_sb[:, rk:rk + 1], in1=kd[:, :, 0:128], op0=ALU.mult, op1=ALU.add),
                             r=[rHALO, rCONST], w=[rKD])
                        R.op("dve", lambda h, rk=rk: h.scalar_tensor_tensor(out=kd[:, :, 17 * 128:18 * 128], in0=halo_sb[:, 0, :].rearrange("p (g t) -> p g t", g=2), scalar=oh_sb[:, 8 + rk:9 + rk], in1=kd[:, :, 17 * 128:18 * 128], op0=ALU.mult, op1=ALU.add),
                             r=[rHALO, rCONST], w=[rKD])
                        R.op("dve", lambda h, rk=rk: h.scalar_tensor_tensor(out=v_sb[:, 0, :], in0=halo_sb[:, 2, 128:256], scalar=oh_sb[:, rk:rk + 1], in1=v_sb[:, 0, :], op0=ALU.mult, op1=ALU.add),
                             r=[rHALO, rCONST], w=[rV])
                        R.op("dve", lambda h, rk=rk: h.scalar_tensor_tensor(out=v_sb[:, 17, :], in0=halo_sb[:, 2, 0:128], scalar=oh_sb[:, 8 + rk:9 + rk], in1=v_sb[:, 17, :], op0=ALU.mult, op1=ALU.add),
                             r=[rHALO, rCONST], w=[rV])

                if u == 0 and l == 0:
                    for l2 in range(1, L):
                        cast_layer(l2)
                R.op("sp", lambda h, l=l: h.dma_start(out=wbt[:], in_=wb_out[l].rearrange("(c p) n -> p c n", p=128)), w=[rWB], dma=True, extra=castops[("out", l)])

                R.mark(f'u{u} l{l} ag done')
                for t in range(0 if KSTOP == 'inproj' else 4):
                    ts = slice(t * 512, (t + 1) * 512)
                    for jj in range(4):
                        j = t * 4 + jj
                        kbs = [kb for kb in (j - 1, j, j + 1) if prompt or 0 <= kb < 16]
                        for ki, kb in enumerate(kbs):
                            o = kb - j + 1
                            bl = nb(); bu = nb()
                            def smm(h, kb=kb, jj=jj, t=t, bl=bl, bu=bu):
                                for g in range(2):
                                    h.matmul(bank(bl)[:, g * 256:(g + 1) * 256].rearrange("p (a t) -> p a t", a=2), lhsT=kd[0:64, g, (kb + 1) * 128:(kb + 2) * 128], rhs=q_sb[0:64, t, 2 * g:2 * g + 2, jj * 128:(jj + 1) * 128], start=True, stop=True)
                                    ins = h.matmul(bank(bu)[:, g * 256:(g + 1) * 256].rearrange("p (a t) -> p a t", a=2), lhsT=kd[64:128, g, (kb + 1) * 128:(kb + 2) * 128], rhs=q_sb[64:128, t, 2 * g:2 * g + 2, jj * 128:(jj + 1) * 128], start=True, stop=True)
                                return ins
                            R.op("pe", smm, r=[rKD, rQ[t]], w=[BR[bl], BR[bu]])
                            for hb, bb in ((0, bl), (1, bu)):
                                edge = None
                                if prompt and j == 0 and kb == -1:
                                    edge = 16
                                if prompt and j == 15 and kb == 16:
                                    edge = 17
                                if edge is None:
                                    R.op("dve", lambda h, hb=hb, bb=bb, o=o: h.tensor_tensor(out=sc_sb[:, hb, :], in0=bank(bb), in1=bias_sb[:, o, hb, :], op=ALU.add),
                                         r=[BR[bb], rCONST], w=[rSC[hb]])
                                else:
                                    R.op("dve", lambda h, hb=hb, bb=bb, o=o, edge=edge: h.scalar_tensor_tensor(out=sc_sb[:, hb, :], in0=bank(bb), scalar=oh_sb[:, edge:edge + 1], in1=bias_sb[:, o, hb, :], op0=ALU.add, op1=ALU.add),
                                         r=[BR[bb], rCONST], w=[rSC[hb]])
                                R.op("act", lambda h, hb=hb, ki=ki: h.activation(out=pt_sb[:, hb * 3 + ki, :], in_=sc_sb[:, hb, :], func=AF.Exp),
                                     r=[rSC[hb]], w=[rPT[hb * 3 + ki]])
                        for hb in range(2):
                            bn = nb(); bd = nb()
                            def pv(h, hb=hb, bn=bn, bd=bd, kbs=kbs):
                                for g in range(2):
                                    for ki, kb in enumerate(kbs):
                                        h.matmul(bank(bn)[0:64, g * 256:(g + 1) * 256], lhsT=v_sb[:, kb + 1, g * 64:(g + 1) * 64], rhs=pt_sb[:, hb * 3 + ki, g * 256:(g + 1) * 256], start=(ki == 0), stop=(ki == len(kbs) - 1))
                                for ki, kb in enumerate(kbs):
                                    ins = h.matmul(bank(bd)[0:64, :], lhsT=ones_b[:, 0:64], rhs=pt_sb[:, hb * 3 + ki, :], start=(ki == 0), stop=(ki == len(kbs) - 1))
                                return ins
                            R.op("pe", pv, r=[rV, rMISC] + [rPT[hb * 3 + ki] for ki in range(len(kbs))], w=[BR[bn], BR[bd]])
                            for ge in range(4):
                                hh = 4 * (ge // 2) + 2 * (ge % 2) + hb
                                R.op("dve", lambda h, ge=ge, hh=hh, bd=bd, l=l: h.tensor_scalar(out=rden[0:64, ge * 128:(ge + 1) * 128], in0=bank(bd)[0:64, ge * 128:(ge + 1) * 128], scalar1=sink_sb[0:64, l * 8 + hh:l * 8 + hh + 1], scalar2=None, op0=ALU.add),
                                     r=[BR[bd], rCONST], w=[rRDEN])
                            R.op("act", lambda h: h.activation(out=rden[0:64, :], in_=rden[0:64, :], func=AF.Ln), r=[rRDEN], w=[rRDEN])
                            R.op("act", lambda h: h.activation(out=rden[0:64, :], in_=rden[0:64, :], func=AF.Exp, scale=-1.0), r=[rRDEN], w=[rRDEN])
                            R.op("dve", lambda h, hb=hb, bn=bn, jj=jj: h.tensor_tensor(out=yat[hb * 64:(hb + 1) * 64, :, jj * 128:(jj + 1) * 128], in0=bank(bn)[0:64, :].rearrange("p (c t) -> p c t", c=4), in1=rden[0:64, :].rearrange("p (c t) -> p c t", c=4), op=ALU.mult),
                                 r=[BR[bn], rRDEN], w=rYAT)
                    rms_stats(lambda c: yat[:, c, :], 4, 512, lambda c: [rYAT[c]], 1)
                    for c in range(4):
                        R.op("dve", lambda h, c=c, l=l: h.scalar_tensor_tensor(out=hq[:, c, :], in0=yat[:, c, :], scalar=g_sb[:, gcol(2, l, c):gcol(2, l, c) + 1], in1=rstd[:, 1, :], op0=ALU.mult, op1=ALU.mult),
                             r=[rYAT[c], rRSTD[1], rCONST], w=[rHQ[c]])
                    for mc in range(2):
                        pts = {}
                        for mb in range(2):
                            bl = nb(); bu = nb()
                            def msc(h, mc=mc, mb=mb, t=t, bl=bl, bu=bu):
                                h.matmul(bank(bl), lhsT=km_sb[0:64, mc, mb * 128:(mb + 1) * 128], rhs=qm_sb[0:64, t, mc, :], start=True, stop=True)
                                return h.matmul(bank(bu), lhsT=km_sb[64:128, mc, mb * 128:(mb + 1) * 128], rhs=qm_sb[64:128, t, mc, :], start=True, stop=True)
                            R.op("pe", msc, r=[rKM, rQM[t]], w=[BR[bl], BR[bu]])
                            for hb, bb in ((0, bl), (1, bu)):
                                R.op("act", lambda h, hb=hb, mb=mb, bb=bb: h.activation(out=pt_sb[:, hb * 3 + mb, :], in_=bank(bb), func=AF.Exp), r=[BR[bb]], w=[rPT[hb * 3 + mb]])
                        for hb in range(2):
                            bn = nb(); bd = nb()
                            hm = 2 * mc + hb
                            def mpv(h, hb=hb, hm=hm, bn=bn, bd=bd):
                                for mb in range(2):
                                    h.matmul(bank(bn)[0:64, :], lhsT=vm_sb[:, mb, hm * 64:(hm + 1) * 64], rhs=pt_sb[:, hb * 3 + mb, :], start=(mb == 0), stop=(mb == 1))
                                for mb in range(2):
                                    ins = h.matmul(bank(bd)[0:64, :], lhsT=ones_b[:, 0:64], rhs=pt_sb[:, hb * 3 + mb, :], start=(mb == 0), stop=(mb == 1))
                                return ins
                            R.op("pe", mpv, r=[rVM, rMISC, rPT[hb * 3], rPT[hb * 3 + 1]], w=[BR[bn], BR[bd]])
                            R.op("dve", lambda h, bd=bd: h.tensor_copy(out=rden[0:64, :], in_=bank(bd)[0:64, :]), r=[BR[bd]], w=[rRDEN])
                            R.op("act", lambda h: h.activation(out=rden[0:64, :], in_=rden[0:64, :], func=AF.Ln), r=[rRDEN], w=[rRDEN])
                            R.op("act", lambda h: h.activation(out=rden[0:64, :], in_=rden[0:64, :], func=AF.Exp, scale=-1.0), r=[rRDEN], w=[rRDEN])
                            R.op("dve", lambda h, hb=hb, mc=mc, bn=bn: h.tensor_tensor(out=yat[hb * 64:(hb + 1) * 64, mc, :], in0=bank(bn)[0:64, :], in1=rden[0:64, :], op=ALU.mult),
                                 r=[BR[bn], rRDEN], w=[rYAT[mc]])
                    rms_stats(lambda c: yat[:, c, :], 2, 256, lambda c: [rYAT[c]], 1)
                    for c in range(2):
                        R.op("dve", lambda h, c=c, l=l: h.scalar_tensor_tensor(out=hq[:, 4 + c, :], in0=yat[:, c, :], scalar=g_sb[:, gcol(2, l, 6 + c):gcol(2, l, 6 + c) + 1], in1=rstd[:, 1, :], op0=ALU.mult, op1=ALU.mult),
                             r=[rYAT[c], rRSTD[1], rCONST], w=[rHQ[4 + c]])
                    for oc in range(8):
                        b = nb()
                        def omm(h, oc=oc, b=b):
                            ks = [0, 1, 2, 3, 6, 7]
                            for i, kc in enumerate(ks):
                                ins = h.matmul(bank(b), lhsT=wbt[:, kc, oc * 128:(oc + 1) * 128], rhs=hq[:, i, :], start=(i == 0), stop=(i == 5))
                            return ins
                        R.op("pe", omm, r=[rWB] + rHQ[0:6], w=[BR[b]])
                        R.op("dve", lambda h, oc=oc, ts=ts, b=b: h.tensor_tensor(out=x_sb[:, oc, ts], in0=bank(b), in1=x_sb[:, oc, ts], op=ALU.add), r=[BR[b], rX[oc][t]], w=[rX[oc][t]])

                R.mark(f'u{u} l{l} attn done')
                barrier()
                tabd = tabp_d if prompt else tabs_d
                ngrp = (NPB // 16) if prompt else 1
                for kc in range(0 if KSTOP in ('inproj', 'attn') else 8):
                    bks = [nb() for _ in range(4)]
                    for grp in range(ngrp):
                        if prompt:
                            R.op("sp", lambda h, l=l, grp=grp: h.dma_start(out=u_sb[:], in_=ag_out[l][grp * AGR:grp * AGR + 2048, :].rearrange("(b p) f -> p b f", p=128)),
                                 r=[rAGO[l]], w=[rU], dma=True, ov=True)
                        for sidx in range(2):
                            R.op("sp", lambda h, kc=kc, grp=grp, sidx=sidx: h.dma_start(out=tab_sb[:, sidx, :, :], in_=tabd[sidx, kc, grp * 128:(grp + 1) * 128, :].rearrange("p (b f) -> p b f", b=16)),
                                 w=[rTAB], dma=True, ov=True)
                        def dmm(h, grp=grp, bks=bks):
                            for blk in range(16):
                                first = (grp == 0 and blk == 0); last = (grp == ngrp - 1 and blk == 15)
                                for cc in range(2):
                                    h.matmul(bank(bks[cc])[:, 0:256], lhsT=u_sb[:, blk, cc * 128:(cc + 1) * 128], rhs=tab_sb[:, 0, blk, :], start=first, stop=last)
                                    ins = h.matmul(bank(bks[2 + cc])[:, 0:256], lhsT=u_sb[:, blk, cc * 128:(cc + 1) * 128], rhs=tab_sb[:, 1, blk, :], start=first, stop=last)
                            return ins
                        R.op("pe", dmm, r=[rU, rTAB], w=[BR[b] for b in bks])
                    for cc in range(2):
                        R.op("dve", lambda h, cc=cc, bks=bks: h.tensor_copy(out=cusu[:, 0, cc, :], in_=bank(bks[cc])[:, 0:256]), r=[BR[bks[cc]]], w=[rCUSU])
                        R.op("act", lambda h, cc=cc, bks=bks: h.activation(out=cusu[:, 1, cc, :], in_=bank(bks[2 + cc])[:, 0:256], func=AF.Copy), r=[BR[bks[2 + cc]]], w=[rCUSU])
                    b = nb()
                    def cmm(h, b=b):
                        for cc in range(2):
                            h.matmul(bank(b)[:, cc * 256:(cc + 1) * 256], lhsT=cs_sb[:, 0, :], rhs=cusu[:, 0, cc, :], start=True, stop=False)
                            ins = h.matmul(bank(b)[:, cc * 256:(cc + 1) * 256], lhsT=cs_sb[:, 1, :], rhs=cusu[:, 1, cc, :], start=False, stop=True)
                        return ins
                    R.op("pe", cmm, r=[rCUSU, rCONST], w=[BR[b]])
                    R.op("dve", lambda h, kc=kc, b=b: h.tensor_copy(out=yf_sb[:, :, kc * 256:(kc + 1) * 256], in_=bank(b).rearrange("p (c t) -> p c t", c=2)), r=[BR[b]], w=[rYF])
                for t in range(0 if KSTOP in ('inproj', 'attn') else 4):
                    ts = slice(t * 512, (t + 1) * 512)
                    rms_stats(lambda c, ts=ts: yf_sb[:, c, ts], 2, 256, lambda c: [rYF], 1)
                    for c in range(2):
                        R.op("dve", lambda h, c=c, ts=ts, l=l: h.scalar_tensor_tensor(out=hq[:, c, :], in0=yf_sb[:, c, ts], scalar=g_sb[:, gcol(2, l, 4 + c):gcol(2, l, 4 + c) + 1], in1=rstd[:, 1, :], op0=ALU.mult, op1=ALU.mult),
                             r=[rYF, rRSTD[1], rCONST], w=[rHQ[c]])
                    for oc in range(8):
                        b = nb()
                        def omm2(h, oc=oc, b=b):
                            for i in range(2):
                                ins = h.matmul(bank(b), lhsT=wbt[:, 4 + i, oc * 128:(oc + 1) * 128], rhs=hq[:, i, :], start=(i == 0), stop=(i == 1))
                            return ins
                        R.op("pe", omm2, r=[rWB, rHQ[0], rHQ[1]], w=[BR[b]])
                        R.op("dve", lambda h, oc=oc, ts=ts, b=b: h.tensor_tensor(out=x_sb[:, oc, ts], in0=bank(b), in1=x_sb[:, oc, ts], op=ALU.add), r=[BR[b], rX[oc][t]], w=[rX[oc][t]])

                R.mark(f'u{u} l{l} dft done')
                barrier()
                def ffn_norm(t, l=l):
                    ts = slice(t * 512, (t + 1) * 512)
                    rms_stats(lambda c, ts=ts: x_sb[:, c, ts], 8, D, lambda c, t=t: [rX[c][t]], 0)
                    for c in range(8):
                        R.op("dve", lambda h, c=c, ts=ts, l=l: h.scalar_tensor_tensor(out=h2[:, c, ts], in0=x_sb[:, c, ts], scalar=g_sb[:, gcol(3, l, c):gcol(3, l, c) + 1], in1=rstd[:, 0, :], op0=ALU.mult, op1=ALU.mult),
                             r=[rX[c][t], rRSTD[0], rCONST], w=[rH2[t]])
                ffn_norm(0)
                for qd in range(0 if KSTOP in ('inproj', 'attn', 'dft') else 4):
                    R.op("sp", lambda h, l=l, qd=qd: h.dma_start(out=wa[:, :, 0:1024], in_=wb_f1[l][:, qd * 1024:(qd + 1) * 1024].rearrange("(c p) n -> p c n", p=128)), w=[rWA], dma=True, extra=castops[("f1", l)])
                    R.op("sp", lambda h, l=l, qd=qd: h.dma_start(out=wbt[:], in_=wb_f2[l][qd * 1024:(qd + 1) * 1024, :].rearrange("(c p) n -> p c n", p=128)), w=[rWB], dma=True, extra=castops[("f2", l)])
                    for t in range(4):
                        ts = slice(t * 512, (t + 1) * 512)
                        for fc in range(8):
                            b = nb()
                            def f1(h, fc=fc, ts=ts, b=b):
                                for kc in range(8):
                                    ins = h.matmul(bank(b), lhsT=wa[:, kc, fc * 128:(fc + 1) * 128], rhs=h2[:, kc, ts], start=(kc == 0), stop=(kc == 7))
                                return ins
                            R.op("pe", f1, r=[rWA, rH2[t]], w=[BR[b]])
                            ri = fc % 2
                            R.op("act", lambda h, ri=ri, b=b: h.activation(out=relu_t[:, ri, :], in_=bank(b), func=AF.Relu), r=[BR[b]], w=[rRELU[ri]])
                            R.op("dve", lambda h, ri=ri, fc=fc, b=b: h.tensor_tensor(out=hf[:, fc, :], in0=relu_t[:, ri, :], in1=bank(b), op=ALU.mult), r=[rRELU[ri], BR[b]], w=[rHF])
                        if qd == 0 and t < 3:
                            ffn_norm(t + 1)
                        for oc in range(8):
                            b = nb()
                            def f2(h, oc=oc, b=b):
                                for kc in range(8):
                                    ins = h.matmul(bank(b), lhsT=wbt[:, kc, oc * 128:(oc + 1) * 128], rhs=hf[:, kc, :], start=(kc == 0), stop=(kc == 7))
                                return ins
                            R.op("pe", f2, r=[rWB, rHF], w=[BR[b]])
                            R.op("dve", lambda h, oc=oc, ts=ts, b=b: h.tensor_tensor(out=x_sb[:, oc, ts], in0=bank(b), in1=x_sb[:, oc, ts], op=ALU.add), r=[BR[b], rX[oc][t]], w=[rX[oc][t]])

            R.mark(f'u{u} layers done')
            barrier()
            gf = 4 * L * 8
            for t in range(4):
                ts = slice(t * 512, (t + 1) * 512)
                rms_stats(lambda c, ts=ts: x_sb[:, c, ts], 8, D, lambda c, t=t: [rX[c][t]], 0)
                for c in range(8):
                    R.op("dve", lambda h, c=c, ts=ts: h.scalar_tensor_tensor(out=x_sb[:, c, ts], in0=x_sb[:, c, ts], scalar=g_sb[:, gf + c:gf + c + 1], in1=rstd[:, 0, :], op0=ALU.mult, op1=ALU.mult),
                         r=[rX[c][t], rRSTD[0], rCONST], w=[rX[c][t]])
                for jj in range(4):
                    blk = t * 4 + jj
                    si = blk % 2
                    for half in range(2):
                        b = nb()
                        def trb(h, blk=blk, half=half, b=b):
                            for j in range(4):
                                c = half * 4 + j
                                ins = h.transpose(out=bank(b)[:, j * 128:(j + 1) * 128], in_=x_sb[:, c, blk * 128:(blk + 1) * 128], identity=ident_f[:])
                            return ins
                        R.op("pe", trb, r=[rX[c][t] for c in range(half * 4, half * 4 + 4)] + [rCONST], w=[BR[b]])
                        if half == 0:
                            R.op("dve", lambda h, si=si, b=b: h.tensor_copy(out=stage[:, si, 0:512], in_=bank(b)), r=[BR[b]], w=[rSTG[si]])
                        else:
                            R.op("act", lambda h, si=si, b=b: h.activation(out=stage[:, si, 512:1024], in_=bank(b), func=AF.Copy), r=[BR[b]], w=[rSTG[si]])
                    out_dmas.append(R.op("sp", lambda h, blk=blk, si=si, yout=yout: h.dma_start(out=yout[blk * 128:(blk + 1) * 128, :], in_=stage[:, si, :]), r=[rSTG[si]], w=[], dma=True, ov=True))
        barrier()
        R.op("sp", None, extra=out_dmas)

        for e in ENGS:
            c = 0
            for o in R.q[e]:
                if (not o.dma) and o.sig:
                    c += 1; o.cnt = c

        def semof(d):
            if d.csem is not None:
                return d.csem
            if d.dma:
                return dsem[d.semidx]
            return esem[d.eng]

        def run(e, h):
            for o in R.q[e]:
                for d in o.deps:
                    h.wait_ge(semof(d), d.cnt)
                if o.fn is None:
                    if o.sig:
                        h.nop().then_inc(esem[e], 1)
                    continue
                ins = o.fn(h)
                if o.csem is not None:
                    ins.then_inc(o.csem)
                elif o.dma:
                    ins.then_inc(dsem[o.semidx], 16)
                elif o.sig:
                    ins.then_inc(esem[e], 1)

        @block.tensor
        def _(h):
            run("pe", h)

        @block.scalar
        def _(h):
            run("act", h)

        @block.vector
        def _(h):
            run("dve", h)

        @block.gpsimd
        def _(h):
            run("pool", h)

        @block.sync
        def _(h):
            run("sp", h)
    return nc


def prep_inputs(L, NS, n_prompt_tok, x_prompt, x_sample, mem_prompt, mem_sample, g_mix, w_in, g_mem, w_mem_kv, sinks,
                g_grp, w_out, g_ffn, w_ff1, w_ff2, g_final):
    bias, cs, tab_s, tab_p = host_tables(n_prompt_tok)
    f = lambda a: np.ascontiguousarray(np.asarray(a, dtype=np.float32))
    NG = 4 * L * 8 + 8
    gall = np.zeros((128, NG), np.float32)
    for kind, g in enumerate([g_mix, g_mem, g_grp, g_ffn]):
        g = f(g)
        for l in range(L):
            gall[:, (kind * L + l) * 8:(kind * L + l) * 8 + 8] = g[l].reshape(8, 128).T
    gall[:, 4 * L * 8:] = f(g_final).reshape(8, 128).T
    LW = max(L, 1)
    sk = np.broadcast_to(f(sinks)[:LW].reshape(1, LW * 8), (128, LW * 8)).copy()
    xp = f(x_prompt).reshape(-1, D); xsm = f(x_sample); memp = f(mem_prompt).reshape(-1, D); memsm = f(mem_sample)
    maps = []
    for c in range(NCORE):
        oh = np.zeros((128, 18), np.float32)
        if c > 0:
            oh[:, c - 1] = 1.0
        else:
            oh[:, 16] = NEG
        if c < NCORE - 1:
            oh[:, 8 + c + 1] = 1.0
        else:
            oh[:, 17] = NEG
        maps.append({
            "xp": xp[c * S:(c + 1) * S], "xs": xsm[c * NS:(c + 1) * NS].reshape(NS * S, D),
            "memp": memp, "mems": memsm[c * NS:(c + 1) * NS].reshape(NS * 256, D),
            "w_in": f(w_in)[:LW], "w_kv": f(w_mem_kv)[:LW], "w_out": f(w_out)[:LW], "w_f1": f(w_ff1)[:LW], "w_f2": f(w_ff2)[:LW],
            "gall": gall, "sinks": sk, "oh": oh, "bias": bias, "ident": np.eye(128, dtype=np.float32),
            "cs64": cs, "tab_s": tab_s, "tab_p": tab_p[c],
        })
    return maps


def run(L, NS, inputs):
    n_prompt_tok = inputs["x_prompt"].shape[1]
    nc = build(L, NS, n_prompt_tok // 128)
    maps = prep_inputs(L, NS, n_prompt_tok, **inputs)
    res = run_bass_kernel_spmd(nc, maps, core_ids=list(range(NCORE)))
    yp = np.concatenate([r["yp"] for r in res.results], axis=0)[None]
    ys = np.concatenate([r["ys"].reshape(NS, S, D) for r in res.results], axis=0)
    return yp.astype(np.float32), ys.astype(np.float32)


def kernel(**inputs):
    return run(4, 4, inputs)
```

```python
import os
import types
import numpy as np
import ml_dtypes
import concourse.bass as bass
import concourse.mybir as mybir
from concourse.bass_utils import run_bass_kernel_spmd

F32 = mybir.dt.float32
BF16 = mybir.dt.bfloat16
ALU = mybir.AluOpType
AF = mybir.ActivationFunctionType
AX = mybir.AxisListType
NPBF = ml_dtypes.bfloat16

D = 1024
S = 2048
NCORE = 8
EPS = 1e-6
NEG = -30000.0
HALO = 384
AGR = S + HALO
ENGS = ["pe", "act", "dve", "pool", "sp"]
NDSEM = 24
KSTOP = os.environ.get('KSTOP', '')
KCUT = int(os.environ.get('KCUT', '0'))
KLOG = os.environ.get('KLOG', '')


class Res:
    __slots__ = ("w", "rs")

    def __init__(self):
        self.w = None
        self.rs = []


class Op:
    __slots__ = ("eng", "idx", "fn", "deps", "dma", "sig", "cnt", "known", "semidx", "csem", "ov")


class Rec:
    def __init__(self):
        self.q = {e: [] for e in ENGS}
        self.known = {e: {f: -1 for f in ENGS} for e in ENGS}
        self.wd = {e: set() for e in ENGS}
        self.dmas = []
        self.ovd = []

    def mark(self, name):
        if KLOG:
            print("MARK", name, getattr(self, "nops", 0), flush=True)

    def op(self, eng, fn, r=(), w=(), dma=False, csem=None, ov=False, extra=()):
        self.nops = getattr(self, "nops", 0) + 1
        if KCUT and self.nops > KCUT:
            return None
        extra = [e_ for e_ in extra if e_ is not None]
        if fn is not None and fn.__closure__:
            cells = []
            for c in fn.__closure__:
                try:
                    cells.append(types.CellType(c.cell_contents))
                except ValueError:
                    cells.append(c)
            f2 = types.FunctionType(fn.__code__, fn.__globals__, fn.__name__, fn.__defaults__, tuple(cells))
            f2.__kwdefaults__ = fn.__kwdefaults__
            fn = f2
        o = Op()
        o.eng = eng; o.idx = len(self.q[eng]); o.fn = fn; o.dma = dma or (csem is not None)
        o.sig = False; o.deps = []; o.cnt = 0; o.csem = csem; o.semidx = -1; o.ov = ov
        cand = list(extra)
        for x in r:
            if x.w is not None:
                cand.append(x.w)
        for x in w:
            if x.w is not None:
                cand.append(x.w)
            cand.extend(x.rs)
        if csem is not None:
            o.cnt = 1; o.sig = True
        elif dma:
            n = len(self.dmas)
            o.semidx = n % NDSEM; o.cnt = 16 * (n // NDSEM + 1)
            if n >= NDSEM:
                cand.append(self.dmas[n - NDSEM])
            self.dmas.append(o); o.sig = True
        if o.dma and ov:
            self.ovd.append(o)
        kn = self.known[eng]
        best = {}
        for d in cand:
            if d is o:
                continue
            if d.dma:
                if id(d) not in self.wd[eng]:
                    self.wd[eng].add(id(d)); o.deps.append(d)
                    for f, v in d.known.items():
                        if v > kn[f]:
                            kn[f] = v
            else:
                if d.eng == eng and eng in ("pe", "sp"):
                    continue
                if kn[d.eng] >= d.idx:
                    continue
                if d.eng not in best or best[d.eng].idx < d.idx:
                    best[d.eng] = d
        for d in best.values():
            if kn[d.eng] >= d.idx:
                continue
            d.sig = True; o.deps.append(d)
            kn[d.eng] = d.idx
            for f, v in d.known.items():
                if v > kn[f]:
                    kn[f] = v
        o.known = dict(kn)
        self.q[eng].append(o)
        for x in w:
            x.w = o; x.rs = []
        for x in r:
            if x.w is not o:
                x.rs.append(o)
        return o

    def barrier(self):
        ex = []
        for f in ENGS:
            for o in reversed(self.q[f]):
                if (not o.dma) and o.fn is not None:
                    ex.append(o); break
        pend = list(self.ovd); self.ovd = []
        for e in ENGS:
            self.op(e, None, extra=ex + pend)


def _bf(a):
    return np.ascontiguousarray(a.astype(NPBF))


def host_tables(n_prompt_tok):
    H = 8
    slopes = np.exp2(-8.0 * np.arange(1, H + 1, dtype=np.float64) / H)
    key = np.arange(128)[:, None]; q = np.arange(128)[None, :]
    bias = np.zeros((128, 3, 2, 512), np.float32)
    for o in range(3):
        rel = key + (o - 1) * 128 - q
        ok = np.abs(rel) <= 128
        for hb in range(2):
            for g in range(2):
                for e in range(2):
                    h = 4 * g + 2 * e + hb
                    b = np.where(ok, -slopes[h] * np.abs(rel), NEG)
                    bias[:, o, hb, (g * 2 + e) * 128:(g * 2 + e + 1) * 128] = b
    bias = bias.reshape(128, 3 * 2 * 512)
    c = np.arange(64)
    ang = 2 * np.pi * np.outer(c, c) / 64.0
    C64 = np.cos(ang) / 8.0; S64 = np.sin(ang) / 8.0
    cs = np.zeros((128, 2, 128), np.float64)
    for gg in range(2):
        cs[gg * 64:(gg + 1) * 64, 0, gg * 64:(gg + 1) * 64] = C64
        cs[gg * 64:(gg + 1) * 64, 1, gg * 64:(gg + 1) * 64] = -S64
    cs = _bf(cs.reshape(128, 256))

    def dtab(ntot, k0):
        n = np.arange(ntot, dtype=np.int64)[:, None]
        out = np.empty((2, 8, ntot, 256), NPBF)
        for kc in range(8):
            k = (k0 + kc * 256 + np.arange(256, dtype=np.int64))[None, :]
            a = 2 * np.pi * ((n * k) % ntot).astype(np.float64) / ntot
            out[0, kc] = (np.cos(a) / np.sqrt(ntot)).astype(NPBF)
            out[1, kc] = (np.sin(a) / np.sqrt(ntot)).astype(NPBF)
        ng = ntot // 2048
        out = out.reshape(2, 8, ng, 16, 128, 256).transpose(0, 1, 2, 4, 3, 5)
        return np.ascontiguousarray(out).reshape(2, 8, ng * 128, 4096)
    tab_s = dtab(S, 0)
    tab_p = [dtab(n_prompt_tok, c_ * S) for c_ in range(NCORE)]
    return bias, cs, tab_s, tab_p


def build(L, NS, NPB):
    nc = bass.Bass("TRN2", target_bir_lowering=False)
    R = Rec()
    NU = 1 + NS
    di = lambda n, s, dt=F32: nc.dram_tensor(n, s, dt, kind="ExternalInput").ap()
    xp = di("xp", [S, D]); xs = di("xs", [NS * S, D]); memp = di("memp", [256, D]); mems = di("mems", [NS * 256, D])
    LW = max(L, 1)
    w_in = di("w_in", [LW, D, 1280]); w_kv = di("w_kv", [LW, D, 512]); w_out = di("w_out", [LW, D, D])
    w_f1 = di("w_f1", [LW, D, 4096]); w_f2 = di("w_f2", [LW, 4096, D])
    NG = 4 * L * 8 + 8
    gall = di("gall", [128, NG]); sinks_d = di("sinks", [128, LW * 8]); oh_d = di("oh", [128, 18])
    bias_d = di("bias", [128, 3072]); ident_d = di("ident", [128, 128])
    cs_d = di("cs64", [128, 256], BF16)
    tabs_d = di("tab_s", [2, 8, 128, 4096], BF16); tabp_d = di("tab_p", [2, 8, NPB * 8, 4096], BF16)
    yp = nc.dram_tensor("yp", [S, D], F32, kind="ExternalOutput").ap()
    ys = nc.dram_tensor("ys", [NS * S, D], F32, kind="ExternalOutput").ap()
    dn = lambda n, s: nc.dram_tensor(n, s, BF16, kind="Internal").ap()
    wb_in = dn("wb_in", [LW, D, 1408]); wb_kv = dn("wb_kv", [LW, D, 512]); wb_out = dn("wb_out", [LW, D, D])
    wb_f1 = dn("wb_f1", [LW, D, 4096]); wb_f2 = dn("wb_f2", [LW, 4096, D])
    ag_in = [dn(f"ag_in{l}", [AGR, 256]) for l in range(L)]
    ag_out = [dn(f"ag_out{l}", [NCORE * AGR, 256]) for l in range(L)]

    import contextlib
    es = contextlib.ExitStack()
    with es:
        sb = lambda n, s, dt: es.enter_context(nc.sbuf_tensor(n, s, dt))
        x_sb = sb("x_sb", [128, 8, S], F32)
        wa = sb("wa", [128, 8, 1408], BF16)
        wbt = sb("wbt", [128, 8, 1024], BF16)
        bias_sb = sb("bias_sb", [128, 3, 2, 512], F32)
        ident_f = sb("ident_f", [128, 128], F32)
        ident_b = sb("ident_b", [128, 128], BF16)
        ones_b = sb("ones_b", [128, 128], BF16)
        cs_sb = sb("cs_sb", [128, 2, 128], BF16)
        g_sb = sb("g_sb", [128, NG], F32)
        sink_sb = sb("sink_sb", [128, LW * 8], F32)
        oh_sb = sb("oh_sb", [128, 18], F32)
        small = sb("small", [128, 8], F32)
        OVN = 91 * 512
        ov = sb("ov", [128, OVN], BF16)
        ps = es.enter_context(nc.psum_tensor("ps", [128, 4096], F32))
        esem = {e: es.enter_context(nc.semaphore("s_" + e)) for e in ENGS}
        dsem = [es.enter_context(nc.semaphore(f"d{i}")) for i in range(NDSEM)]
        csems = [es.enter_context(nc.semaphore(f"cc{l}")) for l in range(L)]
        block = es.enter_context(nc.Block())

        K = 512

        def carve(off_k, nbytes_k, dt, pat=None, **kw):
            a = ov[:, int(off_k * K):int((off_k + nbytes_k) * K)]
            if dt == F32:
                a = a.bitcast(F32)
            if pat:
                a = a.rearrange(pat, **kw)
            return a
        hq = carve(0, 8, BF16, "p (c t) -> p c t", c=8)
        q_sb = carve(8, 16, BF16, "p (i c t) -> p i c t", i=4, c=4)
        qm_sb = carve(24, 8, BF16, "p (i c t) -> p i c t", i=4, c=2)
        kd = carve(32, 9, BF16, "p (g t) -> p g t", g=2)
        v_sb = carve(41, 4.5, BF16, "p (b f) -> p b f", b=18)
        u_sb = carve(45.5, 8, BF16, "p (b f) -> p b f", b=16)
        sc_sb = carve(53.5, 4, F32, "p (i t) -> p i t", i=2)
        pt_sb = carve(57.5, 6, BF16, "p (i t) -> p i t", i=6)
        stage = carve(53.5, 8, F32, "p (i t) -> p i t", i=2)
        yat = carve(63.5, 8, F32, "p (c t) -> p c t", c=4)
        rden = carve(71.5, 2, F32)
        memT = carve(73.5, 4, BF16, "p (c t) -> p c t", c=8)
        memg = carve(77.5, 4, BF16, "p (c t) -> p c t", c=8)
        km_sb = carve(81.5, 1, BF16, "p (c t) -> p c t", c=2)
        vm_sb = carve(82.5, 1, BF16, "p (c t) -> p c t", c=2)
        sq_sb = carve(83.5, 2, BF16, "p (i t) -> p i t", i=2)
        rstd = carve(85.5, 4, F32, "p (i t) -> p i t", i=2)
        halo_sb = carve(89.5, 1.5, BF16, "p (i t) -> p i t", i=3)
        tab_sb = carve(8, 16, BF16, "p (s b f) -> p s b f", s=2, b=16)
        cusu = carve(24, 2, BF16, "p (s c f) -> p s c f", s=2, c=2)
        yf_sb = carve(26, 16, F32, "p (c t) -> p c t", c=2)
        h2 = carve(8, 32, BF16, "p (c t) -> p c t", c=8)
        hf = carve(40, 8, BF16, "p (c t) -> p c t", c=8)
        relu_t = carve(48, 4, F32, "p (i t) -> p i t", i=2)

        bank = lambda b: ps[:, b * 512:(b + 1) * 512]
        BR = [Res() for _ in range(8)]
        bctr = [0]

        def nb():
            b = bctr[0] % 8; bctr[0] += 1
            return b
        rX = [[Res() for _ in range(4)] for _ in range(8)]
        rWA = Res(); rWB = Res(); rMISC = Res()
        rHQ = [Res() for _ in range(8)]
        rQ = [Res() for _ in range(4)]; rQM = [Res() for _ in range(4)]
        rKD = Res(); rV = Res(); rU = Res()
        rSC = [Res(), Res()]; rPT = [Res() for _ in range(6)]
        rYAT = [Res() for _ in range(4)]; rRDEN = Res()
        rMEMT = Res(); rMEMG = Res(); rKM = Res(); rVM = Res()
        rSQ = [Res(), Res()]; rRSTD = [Res(), Res()]; rHALO = Res()
        rTAB = Res(); rCUSU = Res(); rYF = Res()
        rTABH = [Res(), Res()]; rUH = [Res(), Res()]; dstep = [0]
        rH2 = [Res() for _ in range(4)]; rHF = Res(); rRELU = [Res(), Res()]
        rSTG = [Res(), Res()]; rSMALL = Res()
        rWD = {}
        rAGI = [Res() for _ in range(L)]; rAGO = [Res() for _ in range(L)]
        rOUT = Res()
        rCONST = Res()
        out_dmas = []

        def ld(dst, src, w, eng="sp", **kw):
            return R.op(eng, lambda h: h.dma_start(out=dst, in_=src), w=w, dma=True, **kw)
        ld(bias_sb[:].rearrange("p a b c -> p (a b c)"), bias_d[:, :], [rCONST])
        ld(ident_f[:], ident_d[:, :], [rCONST]); ld(cs_sb[:].rearrange("p a b -> p (a b)"), cs_d[:, :], [rCONST])
        ld(g_sb[:], gall[:, :], [rCONST]); ld(sink_sb[:], sinks_d[:, :], [rCONST]); ld(oh_sb[:], oh_d[:, :], [rCONST])
        R.op("dve", lambda h: h.tensor_copy(out=ident_b[:], in_=ident_f[:]), r=[rCONST], w=[rMISC])
        R.op("dve", lambda h: h.memset(ones_b[:], 1.0), w=[rMISC])
        R.op("act", lambda h: h.activation(out=sink_sb[:], in_=sink_sb[:], func=AF.Exp), r=[rCONST], w=[rCONST])

        castops = {}

        def cast(dst, src, key):
            castops.setdefault(key, [])
            if os.environ.get('KNOCAST'):
                return
            castops[key].append(R.op("pool", lambda h: h.dma_start(out=dst, in_=src), w=[], dma=True))

        def cast_layer(l):
            for r0 in range(0, D, 256):
                rs = slice(r0, r0 + 256)
                cast(wb_in[l, rs, 0:512], w_in[l, rs, 0:512], ("in", l))
                for g in range(2):
                    for dup in range(2):
                        cast(wb_in[l, rs, 512 + 128 * g + 64 * dup:512 + 128 * g + 64 * dup + 64], w_in[l, rs, 512 + 64 * g:576 + 64 * g], ("in", l))
                cast(wb_in[l, rs, 768:1408], w_in[l, rs, 640:1280], ("in", l))
            for r0 in range(0, D, 256):
                cast(wb_kv[l, r0:r0 + 256, :], w_kv[l, r0:r0 + 256, :], ("kv", l))
            for r0 in range(0, D, 256):
                cast(wb_out[l, r0:r0 + 256, :], w_out[l, r0:r0 + 256, :], ("out", l))
            for r0 in range(0, D, 256):
                cast(wb_f1[l, r0:r0 + 256, :], w_f1[l, r0:r0 + 256, :], ("f1", l))
            for r0 in range(0, 4096, 256):
                cast(wb_f2[l, r0:r0 + 256, :], w_f2[l, r0:r0 + 256, :], ("f2", l))
        if L > 0:
            cast_layer(0)

        gcol = lambda kind, l, c: (kind * L + l) * 8 + c

        def rms_stats(src_fn, nchunks, nfeat, tile_res_r, ri):
            b = nb()
            for c in range(nchunks):
                si = c % 2
                R.op("act", lambda h, c=c, si=si: h.activation(out=sq_sb[:, si, :], in_=src_fn(c), func=AF.Square),
                     r=tile_res_r(c), w=[rSQ[si]])
                R.op("pe", lambda h, c=c, si=si, b=b: h.matmul(bank(b), lhsT=ones_b[:], rhs=sq_sb[:, si, :], start=(c == 0), stop=(c == nchunks - 1)),
                     r=[rSQ[si], rMISC], w=[BR[b]])
            R.op("act", lambda h, b=b: h.activation(out=rstd[:, ri, :], in_=bank(b), func=AF.Ln, bias=EPS, scale=1.0 / nfeat),
                 r=[BR[b]], w=[rRSTD[ri]])
            R.op("act", lambda h: h.activation(out=rstd[:, ri, :], in_=rstd[:, ri, :], func=AF.Exp, scale=-0.5), r=[rRSTD[ri]], w=[rRSTD[ri]])

        def barrier():
            R.barrier()

        for u in range(NU):
            prompt = (u == 0)
            xin = xp if prompt else xs[(u - 1) * S:u * S, :]
            yout = yp if prompt else ys[(u - 1) * S:u * S, :]
            memin = memp if prompt else mems[(u - 1) * 256:u * 256, :]
            barrier()
            R.mark(f'u{u} start')
            for blk in range(16):
                si = blk % 2
                R.op("sp", lambda h, blk=blk, si=si, xin=xin: h.dma_start(out=stage[:, si, :], in_=xin[blk * 128:(blk + 1) * 128, :]),
                     w=[rSTG[si]], dma=True, ov=True)
                for half in range(2):
                    b = nb()
                    def tr(h, blk=blk, si=si, half=half, b=b):
                        for j in range(4):
                            c = half * 4 + j
                            ins = h.transpose(out=bank(b)[:, j * 128:(j + 1) * 128], in_=stage[:, si, c * 128:(c + 1) * 128], identity=ident_f[:])
                        return ins
                    R.op("pe", tr, r=[rSTG[si], rCONST], w=[BR[b]])
                    R.op("dve" if half == 0 else "act",
                         (lambda h, blk=blk, half=half, b=b: h.tensor_copy(out=x_sb[:, half * 4:half * 4 + 4, blk * 128:(blk + 1) * 128], in_=bank(b).rearrange("p (a t) -> p a t", a=4)))
                         if half == 0 else
                         (lambda h, blk=blk, half=half, b=b: h.activation(out=x_sb[:, half * 4:half * 4 + 4, blk * 128:(blk + 1) * 128], in_=bank(b).rearrange("p (a t) -> p a t", a=4), func=AF.Copy)),
                         r=[BR[b]], w=[rX[c][blk // 4] for c in range(half * 4, half * 4 + 4)])
            R.mark(f'u{u} xloaded')
            for mb in range(2):
                R.op("sp", lambda h, mb=mb, memin=memin: h.dma_start(out=stage[:, 0, :], in_=memin[mb * 128:(mb + 1) * 128, :]), w=[rSTG[0]], dma=True, ov=True)
                R.op("act", lambda h: h.activation(out=stage[:, 1, :], in_=stage[:, 0, :], func=AF.Square), r=[rSTG[0]], w=[rSTG[1]])
                R.op("dve", lambda h: h.reduce_sum(out=small[:, 0:1], in_=stage[:, 1, :], axis=AX.X), r=[rSTG[1]], w=[rSMALL])
                R.op("act", lambda h: h.activation(out=small[:, 1:2], in_=small[:, 0:1], func=AF.Sqrt, bias=EPS, scale=1.0 / D), r=[rSMALL], w=[rSMALL])
                R.op("dve", lambda h: h.reciprocal(out=small[:, 2:3], in_=small[:, 1:2]), r=[rSMALL], w=[rSMALL])
                R.op("dve", lambda h: h.tensor_scalar(out=hq[:].rearrange("p c t -> p (c t)")[:, 0:1024], in0=stage[:, 0, :], scalar1=small[:, 2:3], scalar2=None, op0=ALU.mult),
                     r=[rSTG[0], rSMALL], w=rHQ)
                b = nb()
                def trm(h, b=b):
                    bb = bank(b).bitcast(BF16)
                    for c in range(8):
                        ins = h.transpose(out=bb[:, c * 128:(c + 1) * 128], in_=hq[:].rearrange("p c t -> p (c t)")[:, c * 128:(c + 1) * 128], identity=ident_b[:])
                    return ins
                R.op("pe", trm, r=rHQ + [rMISC], w=[BR[b]])
                R.op("dve", lambda h, mb=mb, b=b: h.tensor_copy(out=memT[:, :, mb * 128:(mb + 1) * 128], in_=bank(b).bitcast(BF16).rearrange("p (c t) -> p c t", c=8)),
                     r=[BR[b]], w=[rMEMT])

            R.mark(f'u{u} memdone')
            for l in range(L):
                barrier()
                R.op("sp", lambda h, l=l: h.dma_start(out=wa[:], in_=wb_in[l].rearrange("(c p) n -> p c n", p=128)), w=[rWA], dma=True, extra=castops[("in", l)])
                R.op("sp", lambda h, l=l: h.dma_start(out=wbt[:, :, 0:512], in_=wb_kv[l].rearrange("(c p) n -> p c n", p=128)), w=[rWB], dma=True, extra=castops[("kv", l)])
                for c in range(8):
                    R.op("dve", lambda h, c=c, l=l: h.tensor_scalar(out=memg[:, c, :], in0=memT[:, c, :], scalar1=g_sb[:, gcol(1, l, c):gcol(1, l, c) + 1], scalar2=None, op0=ALU.mult),
                         r=[rMEMT, rCONST], w=[rMEMG])
                b = nb()
                def kmm(h, b=b):
                    for cc in range(2):
                        for kc in range(8):
                            ins = h.matmul(bank(b)[:, cc * 256:(cc + 1) * 256], lhsT=wbt[:, kc, cc * 128:(cc + 1) * 128], rhs=memg[:, kc, :], start=(kc == 0), stop=(kc == 7))
                    return ins
                R.op("pe", kmm, r=[rWB, rMEMG], w=[BR[b]])
                R.op("dve", lambda h, b=b: h.tensor_copy(out=km_sb[:].rearrange("p c t -> p (c t)"), in_=bank(b)), r=[BR[b]], w=[rKM])
                b = nb()
                def vmm(h, b=b):
                    for mb in range(2):
                        for kc in range(8):
                            ins = h.matmul(bank(b)[:, mb * 256:(mb + 1) * 256], lhsT=memg[:, kc, mb * 128:(mb + 1) * 128], rhs=wbt[:, kc, 256:512], start=(kc == 0), stop=(kc == 7))
                    return ins
                R.op("pe", vmm, r=[rWB, rMEMG], w=[BR[b]])
                R.op("act", lambda h, b=b: h.activation(out=vm_sb[:].rearrange("p c t -> p (c t)"), in_=bank(b), func=AF.Copy), r=[BR[b]], w=[rVM])

                R.mark(f'u{u} l{l} memkv done')
                for t in range(4):
                    ts = slice(t * 512, (t + 1) * 512)
                    rms_stats(lambda c, ts=ts: x_sb[:, c, ts], 8, D, lambda c, t=t: [rX[c][t]], 0)
                    for c in range(8):
                        R.op("dve", lambda h, c=c, ts=ts, l=l: h.scalar_tensor_tensor(out=hq[:, c, :], in0=x_sb[:, c, ts], scalar=g_sb[:, gcol(0, l, c):gcol(0, l, c) + 1], in1=rstd[:, 0, :], op0=ALU.mult, op1=ALU.mult),
                             r=[rX[c][t], rRSTD[0], rCONST], w=[rHQ[c]])
                    R.mark(f'u{u} l{l} t{t} hq done')
                    outs = [("q", i) for i in range(4)] + [("k", i) for i in range(2)] + [("m", i) for i in range(2)]
                    for kind, i in outs:
                        col = {"q": i * 128, "k": 512 + i * 128, "m": 1152 + i * 128}[kind]
                        b = nb()
                        def mm(h, col=col, b=b):
                            for kc in range(8):
                                ins = h.matmul(bank(b), lhsT=wa[:, kc, col:col + 128], rhs=hq[:, kc, :], start=(kc == 0), stop=(kc == 7))
                            return ins
                        R.op("pe", mm, r=[rWA] + rHQ, w=[BR[b]])
                        if kind == "q":
                            R.op("act", lambda h, i=i, t=t, b=b: h.activation(out=q_sb[:, t, i, :], in_=bank(b), func=AF.Copy, scale=0.125), r=[BR[b]], w=[rQ[t]])
                        elif kind == "m":
                            R.op("act", lambda h, i=i, t=t, b=b: h.activation(out=qm_sb[:, t, i, :], in_=bank(b), func=AF.Copy, scale=0.125), r=[BR[b]], w=[rQM[t]])
                        else:
                            R.op("dve", lambda h, i=i, t=t, b=b: h.tensor_copy(out=kd[:, i, 128 + t * 512:128 + (t + 1) * 512], in_=bank(b)), r=[BR[b]], w=[rKD])
                    R.mark(f'u{u} l{l} t{t} fm done')
                    for j in range(4):
                        blk = t * 4 + j
                        b = nb()
                        def mmv(h, j=j, b=b):
                            for kc in range(8):
                                ins = h.matmul(bank(b)[:, 0:384], lhsT=hq[:, kc, j * 128:(j + 1) * 128], rhs=wa[:, kc, 768:1152], start=(kc == 0), stop=(kc == 7))
                            return ins
                        R.op("pe", mmv, r=[rWA] + rHQ, w=[BR[b]])
                        R.op("dve", lambda h, blk=blk, b=b: h.tensor_copy(out=v_sb[:, 1 + blk, :], in_=bank(b)[:, 0:128]), r=[BR[b]], w=[rV])
                        R.op("dve", lambda h, blk=blk, b=b: h.tensor_copy(out=u_sb[:, blk, :], in_=bank(b)[:, 128:384]), r=[BR[b]], w=[rU])

                R.mark(f'u{u} l{l} inproj done')
                if prompt and not os.environ.get('KNOAG'):
                    R.op("sp", lambda h, l=l: h.dma_start(out=ag_in[l][0:S, :].rearrange("(p b) f -> p b f", p=128), in_=u_sb[:]), r=[rU], w=[rAGI[l]], dma=True, ov=True)
                    R.op("sp", lambda h, l=l: h.dma_start(out=ag_in[l][S:S + 128, :].rearrange("p (g t) -> p g t", g=2), in_=kd[:, :, 128:256]), r=[rKD], w=[rAGI[l]], dma=True, ov=True)
                    R.op("sp", lambda h, l=l: h.dma_start(out=ag_in[l][S + 128:S + 256, :].rearrange("p (g t) -> p g t", g=2), in_=kd[:, :, 128 + 15 * 128:128 + 16 * 128]), r=[rKD], w=[rAGI[l]], dma=True, ov=True)
                    R.op("sp", lambda h, l=l: h.dma_start(out=ag_in[l][S + 256:S + 384, 0:128], in_=v_sb[:, 1, :]), r=[rV], w=[rAGI[l]], dma=True, ov=True)
                    R.op("sp", lambda h, l=l: h.dma_start(out=ag_in[l][S + 256:S + 384, 128:256], in_=v_sb[:, 16, :]), r=[rV], w=[rAGI[l]], dma=True, ov=True)
                    agw = [o for o in R.q["sp"][-5:]]
                    R.op("pool", lambda h, l=l: h.collective_compute("AllGather", ALU.bypass, replica_groups=[list(range(NCORE))], ins=[ag_in[l]], outs=[ag_out[l]]),
                         r=[rAGI[l]], w=[rAGO[l]], csem=csems[l], extra=agw)
                    R.op("dve", lambda h: h.memset(kd[:, :, 0:128], 0.0), w=[rKD])
                    R.op("dve", lambda h: h.memset(kd[:, :, 17 * 128:18 * 128], 0.0), w=[rKD])
                    R.op("dve", lambda h: h.memset(v_sb[:, 0, :], 0.0), w=[rV])
                    R.op("dve", lambda h: h.memset(v_sb[:, 17, :], 0.0), w=[rV])
                    for rk in range(NCORE):
                        R.op("sp", lambda h, l=l, rk=rk: h.dma_start(out=halo_sb[:], in_=ag_out[l][rk * AGR + S:rk * AGR + S + 384, :].rearrange("(i p) f -> p i f", p=128)),
                             r=[rAGO[l]], w=[rHALO], dma=True, ov=True)
                        R.op("dve", lambda h, rk=rk: h.scalar_tensor_tensor(out=kd[:, :, 0:128], in0=halo_sb[:, 1, :].rearrange("p (g t) -> p g t", g=2), scalar=oh_sb[:, rk:rk + 1], in1=kd[:, :, 0:128], op0=ALU.mult, op1=ALU.add),
                             r=[rHALO, rCONST], w=[rKD])
                        R.op("dve", lambda h, rk=rk: h.scalar_tensor_tensor(out=kd[:, :, 17 * 128:18 * 128], in0=halo_sb[:, 0, :].rearrange("p (g t) -> p g t", g=2), scalar=oh_sb[:, 8 + rk:9 + rk], in1=kd[:, :, 17 * 128:18 * 128], op0=ALU.mult, op1=ALU.add),
                             r=[rHALO, rCONST], w=[rKD])
                        R.op("dve", lambda h, rk=rk: h.scalar_tensor_tensor(out=v_sb[:, 0, :], in0=halo_sb[:, 2, 128:256], scalar=oh_sb[:, rk:rk + 1], in1=v_sb[:, 0, :], op0=ALU.mult, op1=ALU.add),
                             r=[rHALO, rCONST], w=[rV])
                        R.op("dve", lambda h, rk=rk: h.scalar_tensor_tensor(out=v_sb[:, 17, :], in0=halo_sb[:, 2, 0:128], scalar=oh_sb[:, 8 + rk:9 + rk], in1=v_sb[:, 17, :], op0=ALU.mult, op1=ALU.add),
                             r=[rHALO, rCONST], w=[rV])

                if u == 0 and l == 0:
                    for l2 in range(1, L):
                        cast_layer(l2)
                R.op("sp", lambda h, l=l: h.dma_start(out=wbt[:], in_=wb_out[l].rearrange("(c p) n -> p c n", p=128)), w=[rWB], dma=True, extra=castops[("out", l)])

                R.mark(f'u{u} l{l} ag done')
                for t in range(0 if KSTOP == 'inproj' else 4):
                    ts = slice(t * 512, (t + 1) * 512)
                    for jj in range(4):
                        j = t * 4 + jj
                        kbs = [kb for kb in (j - 1, j, j + 1) if prompt or 0 <= kb < 16]
                        for ki, kb in enumerate(kbs):
                            o = kb - j + 1
                            bl = nb(); bu = nb()
                            def smm(h, kb=kb, jj=jj, t=t, bl=bl, bu=bu):
                                for g in range(2):
                                    h.matmul(bank(bl)[:, g * 256:(g + 1) * 256].rearrange("p (a t) -> p a t", a=2), lhsT=kd[0:64, g, (kb + 1) * 128:(kb + 2) * 128], rhs=q_sb[0:64, t, 2 * g:2 * g + 2, jj * 128:(jj + 1) * 128], start=True, stop=True)
                                    ins = h.matmul(bank(bu)[:, g * 256:(g + 1) * 256].rearrange("p (a t) -> p a t", a=2), lhsT=kd[64:128, g, (kb + 1) * 128:(kb + 2) * 128], rhs=q_sb[64:128, t, 2 * g:2 * g + 2, jj * 128:(jj + 1) * 128], start=True, stop=True)
                                return ins
                            R.op("pe", smm, r=[rKD, rQ[t]], w=[BR[bl], BR[bu]])
                            for hb, bb in ((0, bl), (1, bu)):
                                edge = None
                                if prompt and j == 0 and kb == -1:
                                    edge = 16
                                if prompt and j == 15 and kb == 16:
                                    edge = 17
                                if edge is None:
                                    R.op("dve", lambda h, hb=hb, bb=bb, o=o: h.tensor_tensor(out=sc_sb[:, hb, :], in0=bank(bb), in1=bias_sb[:, o, hb, :], op=ALU.add),
                                         r=[BR[bb], rCONST], w=[rSC[hb]])
                                else:
                                    R.op("dve", lambda h, hb=hb, bb=bb, o=o, edge=edge: h.scalar_tensor_tensor(out=sc_sb[:, hb, :], in0=bank(bb), scalar=oh_sb[:, edge:edge + 1], in1=bias_sb[:, o, hb, :], op0=ALU.add, op1=ALU.add),
                                         r=[BR[bb], rCONST], w=[rSC[hb]])
                                R.op("act", lambda h, hb=hb, ki=ki: h.activation(out=pt_sb[:, hb * 3 + ki, :], in_=sc_sb[:, hb, :], func=AF.Exp),
                                     r=[rSC[hb]], w=[rPT[hb * 3 + ki]])
                        for hb in range(2):
                            bn = nb(); bd = nb()
                            def pv(h, hb=hb, bn=bn, bd=bd, kbs=kbs):
                                for g in range(2):
                                    for ki, kb in enumerate(kbs):
                                        h.matmul(bank(bn)[0:64, g * 256:(g + 1) * 256], lhsT=v_sb[:, kb + 1, g * 64:(g + 1) * 64], rhs=pt_sb[:, hb * 3 + ki, g * 256:(g + 1) * 256], start=(ki == 0), stop=(ki == len(kbs) - 1))
                                for ki, kb in enumerate(kbs):
                                    ins = h.matmul(bank(bd)[0:64, :], lhsT=ones_b[:, 0:64], rhs=pt_sb[:, hb * 3 + ki, :], start=(ki == 0), stop=(ki == len(kbs) - 1))
                                return ins
                            R.op("pe", pv, r=[rV, rMISC] + [rPT[hb * 3 + ki] for ki in range(len(kbs))], w=[BR[bn], BR[bd]])
                            for ge in range(4):
                                hh = 4 * (ge // 2) + 2 * (ge % 2) + hb
                                R.op("dve", lambda h, ge=ge, hh=hh, bd=bd, l=l: h.tensor_scalar(out=rden[0:64, ge * 128:(ge + 1) * 128], in0=bank(bd)[0:64, ge * 128:(ge + 1) * 128], scalar1=sink_sb[0:64, l * 8 + hh:l * 8 + hh + 1], scalar2=None, op0=ALU.add),
                                     r=[BR[bd], rCONST], w=[rRDEN])
                            R.op("act", lambda h: h.activation(out=rden[0:64, :], in_=rden[0:64, :], func=AF.Ln), r=[rRDEN], w=[rRDEN])
                            R.op("act", lambda h: h.activation(out=rden[0:64, :], in_=rden[0:64, :], func=AF.Exp, scale=-1.0), r=[rRDEN], w=[rRDEN])
                            R.op("dve", lambda h, hb=hb, bn=bn, jj=jj: h.tensor_tensor(out=yat[hb * 64:(hb + 1) * 64, :, jj * 128:(jj + 1) * 128], in0=bank(bn)[0:64, :].rearrange("p (c t) -> p c t", c=4), in1=rden[0:64, :].rearrange("p (c t) -> p c t", c=4), op=ALU.mult),
                                 r=[BR[bn], rRDEN], w=rYAT)
                    rms_stats(lambda c: yat[:, c, :], 4, 512, lambda c: [rYAT[c]], 1)
                    for c in range(4):
                        R.op("dve", lambda h, c=c, l=l: h.scalar_tensor_tensor(out=hq[:, c, :], in0=yat[:, c, :], scalar=g_sb[:, gcol(2, l, c):gcol(2, l, c) + 1], in1=rstd[:, 1, :], op0=ALU.mult, op1=ALU.mult),
                             r=[rYAT[c], rRSTD[1], rCONST], w=[rHQ[c]])
                    for mc in range(2):
                        pts = {}
                        for mb in range(2):
                            bl = nb(); bu = nb()
                            def msc(h, mc=mc, mb=mb, t=t, bl=bl, bu=bu):
                                h.matmul(bank(bl), lhsT=km_sb[0:64, mc, mb * 128:(mb + 1) * 128], rhs=qm_sb[0:64, t, mc, :], start=True, stop=True)
                                return h.matmul(bank(bu), lhsT=km_sb[64:128, mc, mb * 128:(mb + 1) * 128], rhs=qm_sb[64:128, t, mc, :], start=True, stop=True)
                            R.op("pe", msc, r=[rKM, rQM[t]], w=[BR[bl], BR[bu]])
                            for hb, bb in ((0, bl), (1, bu)):
                                R.op("act", lambda h, hb=hb, mb=mb, bb=bb: h.activation(out=pt_sb[:, hb * 3 + mb, :], in_=bank(bb), func=AF.Exp), r=[BR[bb]], w=[rPT[hb * 3 + mb]])
                        for hb in range(2):
                            bn = nb(); bd = nb()
                            hm = 2 * mc + hb
                            def mpv(h, hb=hb, hm=hm, bn=bn, bd=bd):
                                for mb in range(2):
                                    h.matmul(bank(bn)[0:64, :], lhsT=vm_sb[:, mb, hm * 64:(hm + 1) * 64], rhs=pt_sb[:, hb * 3 + mb, :], start=(mb == 0), stop=(mb == 1))
                                for mb in range(2):
                                    ins = h.matmul(bank(bd)[0:64, :], lhsT=ones_b[:, 0:64], rhs=pt_sb[:, hb * 3 + mb, :], start=(mb == 0), stop=(mb == 1))
                                return ins
                            R.op("pe", mpv, r=[rVM, rMISC, rPT[hb * 3], rPT[hb * 3 + 1]], w=[BR[bn], BR[bd]])
                            R.op("dve", lambda h, bd=bd: h.tensor_copy(out=rden[0:64, :], in_=bank(bd)[0:64, :]), r=[BR[bd]], w=[rRDEN])
                            R.op("act", lambda h: h.activation(out=rden[0:64, :], in_=rden[0:64, :], func=AF.Ln), r=[rRDEN], w=[rRDEN])
                            R.op("act", lambda h: h.activation(out=rden[0:64, :], in_=rden[0:64, :], func=AF.Exp, scale=-1.0), r=[rRDEN], w=[rRDEN])
                            R.op("dve", lambda h, hb=hb, mc=mc, bn=bn: h.tensor_tensor(out=yat[hb * 64:(hb + 1) * 64, mc, :], in0=bank(bn)[0:64, :], in1=rden[0:64, :], op=ALU.mult),
                                 r=[BR[bn], rRDEN], w=[rYAT[mc]])
                    rms_stats(lambda c: yat[:, c, :], 2, 256, lambda c: [rYAT[c]], 1)
                    for c in range(2):
                        R.op("dve", lambda h, c=c, l=l: h.scalar_tensor_tensor(out=hq[:, 4 + c, :], in0=yat[:, c, :], scalar=g_sb[:, gcol(2, l, 6 + c):gcol(2, l, 6 + c) + 1], in1=rstd[:, 1, :], op0=ALU.mult, op1=ALU.mult),
                             r=[rYAT[c], rRSTD[1], rCONST], w=[rHQ[4 + c]])
                    for oc in range(8):
                        b = nb()
                        def omm(h, oc=oc, b=b):
                            ks = [0, 1, 2, 3, 6, 7]
                            for i, kc in enumerate(ks):
                                ins = h.matmul(bank(b), lhsT=wbt[:, kc, oc * 128:(oc + 1) * 128], rhs=hq[:, i, :], start=(i == 0), stop=(i == 5))
                            return ins
                        R.op("pe", omm, r=[rWB] + rHQ[0:6], w=[BR[b]])
                        R.op("dve", lambda h, oc=oc, ts=ts, b=b: h.tensor_tensor(out=x_sb[:, oc, ts], in0=bank(b), in1=x_sb[:, oc, ts], op=ALU.add), r=[BR[b], rX[oc][t]], w=[rX[oc][t]])

                R.mark(f'u{u} l{l} attn done')
                barrier()
                tabd = tabp_d if prompt else tabs_d
                ngrp = (NPB // 16) if prompt else 1
                for kc in range(0 if KSTOP in ('inproj', 'attn') else 8):
                    bks = [nb() for _ in range(4)]
                    for grp in range(ngrp):
                        if prompt:
                            R.op("sp", lambda h, l=l, grp=grp: h.dma_start(out=u_sb[:], in_=ag_out[l][grp * AGR:grp * AGR + 2048, :].rearrange("(p b) f -> p b f", p=128)),
                                 r=[rAGO[l]], w=[rU], dma=True, ov=True)
                        for sidx in range(2):
                            R.op("sp", lambda h, kc=kc, grp=grp, sidx=sidx: h.dma_start(out=tab_sb[:, sidx, :, :], in_=tabd[sidx, kc, grp * 128:(grp + 1) * 128, :].rearrange("p (b f) -> p b f", b=16)),
                                 w=[rTAB], dma=True, ov=True)
                        def dmm(h, grp=grp, bks=bks):
                            for blk in range(16):
                                first = (grp == 0 and blk == 0); last = (grp == ngrp - 1 and blk == 15)
                                for cc in range(2):
                                    h.matmul(bank(bks[cc])[:, 0:256], lhsT=u_sb[:, blk, cc * 128:(cc + 1) * 128], rhs=tab_sb[:, 0, blk, :], start=first, stop=last)
                                    ins = h.matmul(bank(bks[2 + cc])[:, 0:256], lhsT=u_sb[:, blk, cc * 128:(cc + 1) * 128], rhs=tab_sb[:, 1, blk, :], start=first, stop=last)
                            return ins
                        R.op("pe", dmm, r=[rU, rTAB], w=[BR[b] for b in bks])
                    for cc in range(2):
                        R.op("dve", lambda h, cc=cc, bks=bks: h.tensor_copy(out=cusu[:, 0, cc, :], in_=bank(bks[cc])[:, 0:256]), r=[BR[bks[cc]]], w=[rCUSU])
                        R.op("act", lambda h, cc=cc, bks=bks: h.activation(out=cusu[:, 1, cc, :], in_=bank(bks[2 + cc])[:, 0:256], func=AF.Copy), r=[BR[bks[2 + cc]]], w=[rCUSU])
                    b = nb()
                    def cmm(h, b=b):
                        for cc in range(2):
                            h.matmul(bank(b)[:, cc * 256:(cc + 1) * 256], lhsT=cs_sb[:, 0, :], rhs=cusu[:, 0, cc, :], start=True, stop=False)
                            ins = h.matmul(bank(b)[:, cc * 256:(cc + 1) * 256], lhsT=cs_sb[:, 1, :], rhs=cusu[:, 1, cc, :], start=False, stop=True)
                        return ins
                    R.op("pe", cmm, r=[rCUSU, rCONST], w=[BR[b]])
                    R.op("dve", lambda h, kc=kc, b=b: h.tensor_copy(out=yf_sb[:, :, kc * 256:(kc + 1) * 256], in_=bank(b).rearrange("p (c t) -> p c t", c=2)), r=[BR[b]], w=[rYF])
                for t in range(0 if KSTOP in ('inproj', 'attn') else 4):
                    ts = slice(t * 512, (t + 1) * 512)
                    rms_stats(lambda c, ts=ts: yf_sb[:, c, ts], 2, 256, lambda c: [rYF], 1)
                    for c in range(2):
                        R.op("dve", lambda h, c=c, ts=ts, l=l: h.scalar_tensor_tensor(out=hq[:, c, :], in0=yf_sb[:, c, ts], scalar=g_sb[:, gcol(2, l, 4 + c):gcol(2, l, 4 + c) + 1], in1=rstd[:, 1, :], op0=ALU.mult, op1=ALU.mult),
                             r=[rYF, rRSTD[1], rCONST], w=[rHQ[c]])
                    for oc in range(8):
                        b = nb()
                        def omm2(h, oc=oc, b=b):
                            for i in range(2):
                                ins = h.matmul(bank(b), lhsT=wbt[:, 4 + i, oc * 128:(oc + 1) * 128], rhs=hq[:, i, :], start=(i == 0), stop=(i == 1))
                            return ins
                        R.op("pe", omm2, r=[rWB, rHQ[0], rHQ[1]], w=[BR[b]])
                        R.op("dve", lambda h, oc=oc, ts=ts, b=b: h.tensor_tensor(out=x_sb[:, oc, ts], in0=bank(b), in1=x_sb[:, oc, ts], op=ALU.add), r=[BR[b], rX[oc][t]], w=[rX[oc][t]])

                R.mark(f'u{u} l{l} dft done')
                barrier()
                def ffn_norm(t, l=l):
                    ts = slice(t * 512, (t + 1) * 512)
                    rms_stats(lambda c, ts=ts: x_sb[:, c, ts], 8, D, lambda c, t=t: [rX[c][t]], 0)
                    for c in range(8):
                        R.op("dve", lambda h, c=c, ts=ts, l=l: h.scalar_tensor_tensor(out=h2[:, c, ts], in0=x_sb[:, c, ts], scalar=g_sb[:, gcol(3, l, c):gcol(3, l, c) + 1], in1=rstd[:, 0, :], op0=ALU.mult, op1=ALU.mult),
                             r=[rX[c][t], rRSTD[0], rCONST], w=[rH2[t]])
                ffn_norm(0)
                for qd in range(0 if KSTOP in ('inproj', 'attn', 'dft') else 4):
                    R.op("sp", lambda h, l=l, qd=qd: h.dma_start(out=wa[:, :, 0:1024], in_=wb_f1[l][:, qd * 1024:(qd + 1) * 1024].rearrange("(c p) n -> p c n", p=128)), w=[rWA], dma=True, extra=castops[("f1", l)])
                    R.op("sp", lambda h, l=l, qd=qd: h.dma_start(out=wbt[:], in_=wb_f2[l][qd * 1024:(qd + 1) * 1024, :].rearrange("(c p) n -> p c n", p=128)), w=[rWB], dma=True, extra=castops[("f2", l)])
                    for t in range(4):
                        ts = slice(t * 512, (t + 1) * 512)
                        for fc in range(8):
                            b = nb()
                            def f1(h, fc=fc, ts=ts, b=b):
                                for kc in range(8):
                                    ins = h.matmul(bank(b), lhsT=wa[:, kc, fc * 128:(fc + 1) * 128], rhs=h2[:, kc, ts], start=(kc == 0), stop=(kc == 7))
                                return ins
                            R.op("pe", f1, r=[rWA, rH2[t]], w=[BR[b]])
                            ri = fc % 2
                            R.op("act", lambda h, ri=ri, b=b: h.activation(out=relu_t[:, ri, :], in_=bank(b), func=AF.Relu), r=[BR[b]], w=[rRELU[ri]])
                            R.op("dve", lambda h, ri=ri, fc=fc, b=b: h.tensor_tensor(out=hf[:, fc, :], in0=relu_t[:, ri, :], in1=bank(b), op=ALU.mult), r=[rRELU[ri], BR[b]], w=[rHF])
                        if qd == 0 and t < 3:
                            ffn_norm(t + 1)
                        for oc in range(8):
                            b = nb()
                            def f2(h, oc=oc, b=b):
                                for kc in range(8):
                                    ins = h.matmul(bank(b), lhsT=wbt[:, kc, oc * 128:(oc + 1) * 128], rhs=hf[:, kc, :], start=(kc == 0), stop=(kc == 7))
                                return ins
                            R.op("pe", f2, r=[rWB, rHF], w=[BR[b]])
                            R.op("dve", lambda h, oc=oc, ts=ts, b=b: h.tensor_tensor(out=x_sb[:, oc, ts], in0=bank(b), in1=x_sb[:, oc, ts], op=ALU.add), r=[BR[b], rX[oc][t]], w=[rX[oc][t]])

            R.mark(f'u{u} layers done')
            barrier()
            gf = 4 * L * 8
            for t in range(4):
                ts = slice(t * 512, (t + 1) * 512)
                rms_stats(lambda c, ts=ts: x_sb[:, c, ts], 8, D, lambda c, t=t: [rX[c][t]], 0)
                for c in range(8):
                    R.op("dve", lambda h, c=c, ts=ts: h.scalar_tensor_tensor(out=x_sb[:, c, ts], in0=x_sb[:, c, ts], scalar=g_sb[:, gf + c:gf + c + 1], in1=rstd[:, 0, :], op0=ALU.mult, op1=ALU.mult),
                         r=[rX[c][t], rRSTD[0], rCONST], w=[rX[c][t]])
                for jj in range(4):
                    blk = t * 4 + jj
                    si = blk % 2
                    for half in range(2):
                        b = nb()
                        def trb(h, blk=blk, half=half, b=b):
                            for j in range(4):
                                c = half * 4 + j
                                ins = h.transpose(out=bank(b)[:, j * 128:(j + 1) * 128], in_=x_sb[:, c, blk * 128:(blk + 1) * 128], identity=ident_f[:])
                            return ins
                        R.op("pe", trb, r=[rX[c][t] for c in range(half * 4, half * 4 + 4)] + [rCONST], w=[BR[b]])
                        if half == 0:
                            R.op("dve", lambda h, si=si, b=b: h.tensor_copy(out=stage[:, si, 0:512], in_=bank(b)), r=[BR[b]], w=[rSTG[si]])
                        else:
                            R.op("act", lambda h, si=si, b=b: h.activation(out=stage[:, si, 512:1024], in_=bank(b), func=AF.Copy), r=[BR[b]], w=[rSTG[si]])
                    out_dmas.append(R.op("sp", lambda h, blk=blk, si=si, yout=yout: h.dma_start(out=yout[blk * 128:(blk + 1) * 128, :], in_=stage[:, si, :]), r=[rSTG[si]], w=[], dma=True, ov=True))
        barrier()
        R.op("sp", None, extra=out_dmas)

        for e in ENGS:
            c = 0
            for o in R.q[e]:
                if (not o.dma) and o.sig:
                    c += 1; o.cnt = c

        def semof(d):
            if d.csem is not None:
                return d.csem
            if d.dma:
                return dsem[d.semidx]
            return esem[d.eng]

        def run(e, h):
            for o in R.q[e]:
                for d in o.deps:
                    h.wait_ge(semof(d), d.cnt)
                if o.fn is None:
                    if o.sig:
                        h.nop().then_inc(esem[e], 1)
                    continue
                ins = o.fn(h)
                if o.csem is not None:
                    ins.then_inc(o.csem)
                elif o.dma:
                    ins.then_inc(dsem[o.semidx], 16)
                elif o.sig:
                    ins.then_inc(esem[e], 1)

        @block.tensor
        def _(h):
            run("pe", h)

        @block.scalar
        def _(h):
            run("act", h)

        @block.vector
        def _(h):
            run("dve", h)

        @block.gpsimd
        def _(h):
            run("pool", h)

        @block.sync
        def _(h):
            run("sp", h)
    return nc


def prep_inputs(L, NS, n_prompt_tok, x_prompt, x_sample, mem_prompt, mem_sample, g_mix, w_in, g_mem, w_mem_kv, sinks,
                g_grp, w_out, g_ffn, w_ff1, w_ff2, g_final):
    bias, cs, tab_s, tab_p = host_tables(n_prompt_tok)
    f = lambda a: np.ascontiguousarray(np.asarray(a, dtype=np.float32))
    NG = 4 * L * 8 + 8
    gall = np.zeros((128, NG), np.float32)
    for kind, g in enumerate([g_mix, g_mem, g_grp, g_ffn]):
        g = f(g)
        for l in range(L):
            gall[:, (kind * L + l) * 8:(kind * L + l) * 8 + 8] = g[l].reshape(8, 128).T
    gall[:, 4 * L * 8:] = f(g_final).reshape(8, 128).T
    LW = max(L, 1)
    sk = np.broadcast_to(f(sinks)[:LW].reshape(1, LW * 8), (128, LW * 8)).copy()
    xp = f(x_prompt).reshape(-1, D); xsm = f(x_sample); memp = f(mem_prompt).reshape(-1, D); memsm = f(mem_sample)
    maps = []
    for c in range(NCORE):
        oh = np.zeros((128, 18), np.float32)
        if c > 0:
            oh[:, c - 1] = 1.0
        else:
            oh[:, 16] = NEG
        if c < NCORE - 1:
            oh[:, 8 + c + 1] = 1.0
        else:
            oh[:, 17] = NEG
        maps.append({
            "xp": xp[c * S:(c + 1) * S], "xs": xsm[c * NS:(c + 1) * NS].reshape(NS * S, D),
            "memp": memp, "mems": memsm[c * NS:(c + 1) * NS].reshape(NS * 256, D),
            "w_in": f(w_in)[:LW], "w_kv": f(w_mem_kv)[:LW], "w_out": f(w_out)[:LW], "w_f1": f(w_ff1)[:LW], "w_f2": f(w_ff2)[:LW],
            "gall": gall, "sinks": sk, "oh": oh, "bias": bias, "ident": np.eye(128, dtype=np.float32),
            "cs64": cs, "tab_s": tab_s, "tab_p": tab_p[c],
        })
    return maps


def run(L, NS, inputs):
    n_prompt_tok = inputs["x_prompt"].shape[1]
    nc = build(L, NS, n_prompt_tok // 128)
    maps = prep_inputs(L, NS, n_prompt_tok, **inputs)
    res = run_bass_kernel_spmd(nc, maps, core_ids=list(range(NCORE)))
    yp = np.concatenate([r["yp"] for r in res.results], axis=0)[None]
    ys = np.concatenate([r["ys"].reshape(NS, S, D) for r in res.results], axis=0)
    return yp.astype(np.float32), ys.astype(np.float32)


def kernel(**inputs):
    return run(4, 4, inputs)
```
